# Optimizing a Trainium2 kernel written in Bass

```python
import math
import jax, jax.numpy as jnp
from jax import lax
import numpy as np

D_MODEL = 1024
BATCH = 1
SEQ = 16384
DEPTH = 1
DEC_BATCH = 128
DEC_SEQ = 8
PAST_LEN = 16384
PAGE_SIZE = 128

N_Q_HEADS = 8
N_KV_HEADS = 2
HEAD_DIM = 64
Q_PER_KV = N_Q_HEADS // N_KV_HEADS
ATTN_WIDTH = N_Q_HEADS * HEAD_DIM
KV_WIDTH = N_KV_HEADS * HEAD_DIM
WINDOW = 128
ROPE_THETA = 10000.0
SSM_WIDTH = D_MODEL // 2
SSM_GROUP = 16
N_SSM_GROUPS = SSM_WIDTH // SSM_GROUP
SSM_STATE = 64
DT_MIN = 0.001
DT_MAX = 0.1
N_MEM = 256
N_X_HEADS = 4
X_HEAD_DIM = D_MODEL // N_X_HEADS
D_FF = 2816
RMS_EPS = 1e-6
NEG_INF = -1e30
IN_SPLITS = (ATTN_WIDTH, KV_WIDTH, KV_WIDTH, SSM_WIDTH, D_MODEL, D_MODEL)
IN_WIDTH = sum(IN_SPLITS)

kernel_name = 'hybrid_swa_sink_s5_macaron_decoder_step'


def rms_norm(x, g):
    xf = x.astype(jnp.float32)
    y = xf * lax.rsqrt(jnp.mean(xf * xf, axis=-1, keepdims=True) + RMS_EPS) * g.astype(jnp.float32)
    return y.astype(x.dtype)


def swiglu_ffn(h, w_in, w_out):
    a, b = jnp.split(h @ w_in, 2, axis=-1)
    return (jax.nn.silu(a) * b) @ w_out


def rope(x, pos):
    half = HEAD_DIM // 2
    inv = ROPE_THETA ** (-jnp.arange(half, dtype=jnp.float32) / half)
    ang = pos[:, None] * inv[None, :]
    cos = jnp.cos(ang)[None, :, None, :]
    sin = jnp.sin(ang)[None, :, None, :]
    xf = x.astype(jnp.float32)
    x1, x2 = xf[..., :half], xf[..., half:]
    return jnp.concatenate([x1 * cos - x2 * sin, x2 * cos + x1 * sin], axis=-1).astype(x.dtype)


def band_window_attention(q, k_ctx, v_ctx, k_cur, v_cur, ctx_valid, sinks):
    l = q.shape[1]
    k = jnp.concatenate([k_ctx, k_cur], axis=1)
    v = jnp.concatenate([v_ctx, v_cur], axis=1)
    s = jnp.einsum('nqhgd,nkhd->nhgqk', q, k).astype(jnp.float32) * (HEAD_DIM ** -0.5)
    r = jnp.arange(l)[:, None]
    c = jnp.arange(WINDOW + l)[None, :]
    band = (c >= r) & (c <= WINDOW + r)
    avail = (c >= WINDOW) | ctx_valid[:, None, None]
    mask = (band[None] & avail)[:, None, None]
    s = jnp.where(mask, s, NEG_INF)
    sink = sinks.astype(jnp.float32).reshape(1, N_KV_HEADS, Q_PER_KV, 1, 1)
    m = jnp.maximum(jnp.max(s, axis=-1, keepdims=True), sink)
    p = jnp.exp(s - m)
    p = p / (jnp.sum(p, axis=-1, keepdims=True) + jnp.exp(sink - m))
    return jnp.einsum('nhgqk,nkhd->nqhgd', p.astype(v.dtype), v)


def _complex_linear_combine(e1, e2):
    a1r, a1i, b1r, b1i = e1
    a2r, a2i, b2r, b2i = e2
    return (a2r * a1r - a2i * a1i,
            a2r * a1i + a2i * a1r,
            a2r * b1r - a2i * b1i + b2r,
            a2r * b1i + a2i * b1r + b2i)


def s5_scan(u, h0_re, h0_im, a_re, a_im, log_dt, b_re, b_im, c_re, c_im, d_skip):
    f32 = jnp.float32
    a_re = a_re.astype(f32)
    a_im = a_im.astype(f32)
    dt = jnp.exp(log_dt.astype(f32))[:, None]
    mag = jnp.exp(a_re * dt)
    lb_re = mag * jnp.cos(a_im * dt)
    lb_im = mag * jnp.sin(a_im * dt)
    den = a_re * a_re + a_im * a_im
    nr = lb_re - 1.0
    ni = lb_im
    k_re = (nr * a_re + ni * a_im) / den
    k_im = (ni * a_re - nr * a_im) / den
    b_re = b_re.astype(f32)
    b_im = b_im.astype(f32)
    bb_re = k_re[..., None] * b_re - k_im[..., None] * b_im
    bb_im = k_re[..., None] * b_im + k_im[..., None] * b_re
    x_re = jnp.einsum('btgc,gnc->btgn', u, bb_re)
    x_im = jnp.einsum('btgc,gnc->btgn', u, bb_im)
    x_re = x_re.at[:, 0].add(lb_re * h0_re - lb_im * h0_im)
    x_im = x_im.at[:, 0].add(lb_re * h0_im + lb_im * h0_re)
    a_r = jnp.broadcast_to(lb_re, x_re.shape)
    a_i = jnp.broadcast_to(lb_im, x_re.shape)
    _, _, h_re, h_im = lax.associative_scan(_complex_linear_combine, (a_r, a_i, x_re, x_im), axis=1)
    y = (jnp.einsum('btgn,gcn->btgc', h_re, c_re.astype(f32))
         - jnp.einsum('btgn,gcn->btgc', h_im, c_im.astype(f32))
         + d_skip.astype(f32).reshape(N_SSM_GROUPS, SSM_GROUP) * u)
    return y, h_re[:, -1], h_im[:, -1]


def token_mixing(h, offset, ctx_k, ctx_v, h0_re, h0_im, lp):
    bt, t = h.shape[0], h.shape[1]
    cuts = [int(c) for c in np.cumsum(IN_SPLITS)[:-1]]
    q, k, v, u, gate_a, gate_b = jnp.split(h @ lp['w_in'], cuts, axis=-1)
    pos = jnp.arange(t, dtype=jnp.float32) + offset
    q = rope(q.reshape(bt, t, N_Q_HEADS, HEAD_DIM), pos)
    k = rope(k.reshape(bt, t, N_KV_HEADS, HEAD_DIM), pos)
    v = v.reshape(bt, t, N_KV_HEADS, HEAD_DIM)
    sinks = lp['attn_sinks']
    if ctx_k is None:
        nb = t // WINDOW
        qb = q.reshape(bt * nb, WINDOW, N_KV_HEADS, Q_PER_KV, HEAD_DIM)
        kb = k.reshape(bt, nb, WINDOW, N_KV_HEADS, HEAD_DIM)
        vb = v.reshape(bt, nb, WINDOW, N_KV_HEADS, HEAD_DIM)
        k_prev = jnp.concatenate([jnp.zeros_like(kb[:, :1]), kb[:, :-1]], axis=1)
        v_prev = jnp.concatenate([jnp.zeros_like(vb[:, :1]), vb[:, :-1]], axis=1)
        flat = (bt * nb, WINDOW, N_KV_HEADS, HEAD_DIM)
        valid = jnp.tile(jnp.arange(nb) > 0, bt)
        o = band_window_attention(qb, k_prev.reshape(flat), v_prev.reshape(flat),
                                  kb.reshape(flat), vb.reshape(flat), valid, sinks)
        new_k = k[:, -WINDOW:]
        new_v = v[:, -WINDOW:]
    else:
        qs = q.reshape(bt, t, N_KV_HEADS, Q_PER_KV, HEAD_DIM)
        ctx_k = ctx_k.astype(k.dtype)
        ctx_v = ctx_v.astype(v.dtype)
        o = band_window_attention(qs, ctx_k, ctx_v, k, v, jnp.ones((bt,), dtype=bool), sinks)
        new_k = jnp.concatenate([ctx_k, k], axis=1)[:, -WINDOW:]
        new_v = jnp.concatenate([ctx_v, v], axis=1)[:, -WINDOW:]
    y_a = o.reshape(bt, t, ATTN_WIDTH) @ lp['w_attn_up']
    if h0_re is None:
        h0_re = jnp.zeros((bt, N_SSM_GROUPS, SSM_STATE), jnp.float32)
        h0_im = jnp.zeros((bt, N_SSM_GROUPS, SSM_STATE), jnp.float32)
    y_ssm, h_re, h_im = s5_scan(u.reshape(bt, t, N_SSM_GROUPS, SSM_GROUP).astype(jnp.float32),
                                h0_re.astype(jnp.float32), h0_im.astype(jnp.float32),
                                lp['ssm_a_re'], lp['ssm_a_im'], lp['ssm_log_dt'],
                                lp['ssm_b_re'], lp['ssm_b_im'], lp['ssm_c_re'], lp['ssm_c_im'], lp['ssm_d'])
    z = jax.nn.gelu(y_ssm.reshape(bt, t, SSM_WIDTH)).astype(h.dtype)
    glu_a, glu_b = jnp.split(z @ lp['w_ssm_glu'], 2, axis=-1)
    y_b = glu_a * jax.nn.sigmoid(glu_b)
    merged = jax.nn.sigmoid(gate_a) * y_a + jax.nn.sigmoid(gate_b) * y_b
    return merged @ lp['w_out'], new_k, new_v, h_re, h_im


def memory_kv(mem, g_mem, w_xk, w_xv):
    bt, nm = mem.shape[0], mem.shape[1]
    mn = rms_norm(mem, g_mem)
    mk = (mn @ w_xk).reshape(bt, nm, N_X_HEADS, X_HEAD_DIM)
    mv = (mn @ w_xv).reshape(bt, nm, N_X_HEADS, X_HEAD_DIM)
    return mk, mv


def cross_attention(h, mem_k, mem_v, w_xq, w_xo):
    bt, t = h.shape[0], h.shape[1]
    q = (h @ w_xq).reshape(bt, t, N_X_HEADS, X_HEAD_DIM)
    s = jnp.einsum('bqhd,bkhd->bhqk', q, mem_k.astype(q.dtype)).astype(jnp.float32) * (X_HEAD_DIM ** -0.5)
    p = jax.nn.softmax(s, axis=-1).astype(q.dtype)
    o = jnp.einsum('bhqk,bkhd->bqhd', p, mem_v.astype(q.dtype)).reshape(bt, t, D_MODEL)
    return o @ w_xo


def decoder_layer(x, offset, ctx_k, ctx_v, h0_re, h0_im, mem_k, mem_v, lp):
    x = x + 0.5 * swiglu_ffn(rms_norm(x, lp['g_ffn1']), lp['w_ffn1_in'], lp['w_ffn1_out'])
    mix, new_k, new_v, h_re, h_im = token_mixing(rms_norm(x, lp['g_mix']), offset, ctx_k, ctx_v, h0_re, h0_im, lp)
    x = x + mix
    x = x + cross_attention(rms_norm(x, lp['g_xattn']), mem_k, mem_v, lp['w_xq'], lp['w_xo'])
    x = x + 0.5 * swiglu_ffn(rms_norm(x, lp['g_ffn2']), lp['w_ffn2_in'], lp['w_ffn2_out'])
    return x, new_k, new_v, h_re, h_im


def _normal(k, shape, scale):
    return jax.random.normal(k, shape, jnp.float32) * scale


def _gain(k, shape):
    return 1.0 + 0.02 * jax.random.normal(k, shape, jnp.float32)


def setup_inputs(seed: int = 0) -> dict:
    key = jax.random.key(seed)
    ks = iter(jax.random.split(key, 48))
    L = DEPTH
    G, N, GS = N_SSM_GROUPS, SSM_STATE, SSM_GROUP
    a_re = -0.5 * jnp.exp(0.05 * jax.random.normal(next(ks), (L, G, N), jnp.float32))
    a_im = math.pi * jnp.arange(N, dtype=jnp.float32)[None, None, :] + 0.05 * jax.random.normal(next(ks), (L, G, N), jnp.float32)
    log_dt = jax.random.uniform(next(ks), (L, G), jnp.float32, minval=math.log(DT_MIN), maxval=math.log(DT_MAX))
    return {
        'x_prompt': _normal(next(ks), (BATCH, SEQ, D_MODEL), 1.0),
        'x_sample': _normal(next(ks), (DEC_BATCH, DEC_SEQ, D_MODEL), 1.0),
        'cache_win_k': _normal(next(ks), (L, DEC_BATCH, WINDOW, N_KV_HEADS, HEAD_DIM), 1.0),
        'cache_win_v': _normal(next(ks), (L, DEC_BATCH, WINDOW, N_KV_HEADS, HEAD_DIM), 1.0),
        'state_ssm_re': _normal(next(ks), (L, DEC_BATCH, G, N), 0.3),
        'state_ssm_im': _normal(next(ks), (L, DEC_BATCH, G, N), 0.3),
        'cache_mem_k': _normal(next(ks), (L, DEC_BATCH, N_MEM, N_X_HEADS, X_HEAD_DIM), 1.0),
        'cache_mem_v': _normal(next(ks), (L, DEC_BATCH, N_MEM, N_X_HEADS, X_HEAD_DIM), 1.0),
        'mem_prompt': _normal(next(ks), (BATCH, N_MEM, D_MODEL), 1.0),
        'g_ffn1': _gain(next(ks), (L, D_MODEL)),
        'w_ffn1_in': _normal(next(ks), (L, D_MODEL, 2 * D_FF), D_MODEL ** -0.5),
        'w_ffn1_out': _normal(next(ks), (L, D_FF, D_MODEL), D_FF ** -0.5),
        'g_mix': _gain(next(ks), (L, D_MODEL)),
        'w_in': _normal(next(ks), (L, D_MODEL, IN_WIDTH), D_MODEL ** -0.5),
        'attn_sinks': _normal(next(ks), (L, N_Q_HEADS), 0.5),
        'ssm_a_re': a_re,
        'ssm_a_im': a_im,
        'ssm_log_dt': log_dt,
        'ssm_b_re': _normal(next(ks), (L, G, N, GS), (2 * GS) ** -0.5),
        'ssm_b_im': _normal(next(ks), (L, G, N, GS), (2 * GS) ** -0.5),
        'ssm_c_re': _normal(next(ks), (L, G, GS, N), N ** -0.5),
        'ssm_c_im': _normal(next(ks), (L, G, GS, N), N ** -0.5),
        'ssm_d': _normal(next(ks), (L, SSM_WIDTH), 1.0),
        'w_attn_up': _normal(next(ks), (L, ATTN_WIDTH, D_MODEL), ATTN_WIDTH ** -0.5),
        'w_ssm_glu': _normal(next(ks), (L, SSM_WIDTH, 2 * D_MODEL), SSM_WIDTH ** -0.5),
        'w_out': _normal(next(ks), (L, D_MODEL, D_MODEL), D_MODEL ** -0.5),
        'g_xattn': _gain(next(ks), (L, D_MODEL)),
        'g_mem': _gain(next(ks), (L, D_MODEL)),
        'w_xq': _normal(next(ks), (L, D_MODEL, D_MODEL), D_MODEL ** -0.5),
        'w_xk': _normal(next(ks), (L, D_MODEL, D_MODEL), D_MODEL ** -0.5),
        'w_xv': _normal(next(ks), (L, D_MODEL, D_MODEL), D_MODEL ** -0.5),
        'w_xo': _normal(next(ks), (L, D_MODEL, D_MODEL), D_MODEL ** -0.5),
        'g_ffn2': _gain(next(ks), (L, D_MODEL)),
        'w_ffn2_in': _normal(next(ks), (L, D_MODEL, 2 * D_FF), D_MODEL ** -0.5),
        'w_ffn2_out': _normal(next(ks), (L, D_FF, D_MODEL), D_FF ** -0.5),
        'g_final': _gain(next(ks), (D_MODEL,)),
    }


def reference(x_prompt, x_sample, cache_win_k, cache_win_v, state_ssm_re, state_ssm_im,
              cache_mem_k, cache_mem_v, mem_prompt,
              g_ffn1, w_ffn1_in, w_ffn1_out, g_mix, w_in, attn_sinks,
              ssm_a_re, ssm_a_im, ssm_log_dt, ssm_b_re, ssm_b_im, ssm_c_re, ssm_c_im, ssm_d,
              w_attn_up, w_ssm_glu, w_out, g_xattn, g_mem, w_xq, w_xk, w_xv, w_xo,
              g_ffn2, w_ffn2_in, w_ffn2_out, g_final):
    xp = x_prompt
    xs = x_sample
    wk_p, wv_p, sr_p, si_p, mk_p, mv_p = [], [], [], [], [], []
    wk_s, wv_s, sr_s, si_s = [], [], [], []
    for l in range(DEPTH):
        lp = dict(g_ffn1=g_ffn1[l], w_ffn1_in=w_ffn1_in[l], w_ffn1_out=w_ffn1_out[l],
                  g_mix=g_mix[l], w_in=w_in[l], attn_sinks=attn_sinks[l],
                  ssm_a_re=ssm_a_re[l], ssm_a_im=ssm_a_im[l], ssm_log_dt=ssm_log_dt[l],
                  ssm_b_re=ssm_b_re[l], ssm_b_im=ssm_b_im[l], ssm_c_re=ssm_c_re[l], ssm_c_im=ssm_c_im[l],
                  ssm_d=ssm_d[l], w_attn_up=w_attn_up[l], w_ssm_glu=w_ssm_glu[l], w_out=w_out[l],
                  g_xattn=g_xattn[l], w_xq=w_xq[l], w_xo=w_xo[l],
                  g_ffn2=g_ffn2[l], w_ffn2_in=w_ffn2_in[l], w_ffn2_out=w_ffn2_out[l])
        mem_k_p, mem_v_p = memory_kv(mem_prompt, g_mem[l], w_xk[l], w_xv[l])
        xp, nk, nv, hr, hi = decoder_layer(xp, 0, None, None, None, None, mem_k_p, mem_v_p, lp)
        wk_p.append(nk); wv_p.append(nv); sr_p.append(hr); si_p.append(hi)
        mk_p.append(mem_k_p); mv_p.append(mem_v_p)
        xs, nk, nv, hr, hi = decoder_layer(xs, PAST_LEN, cache_win_k[l], cache_win_v[l],
                                           state_ssm_re[l], state_ssm_im[l],
                                           cache_mem_k[l], cache_mem_v[l], lp)
        wk_s.append(nk); wv_s.append(nv); sr_s.append(hr); si_s.append(hi)
    y_prompt = rms_norm(xp, g_final)
    y_sample = rms_norm(xs, g_final)
    return (y_prompt, y_sample,
            jnp.stack(wk_p), jnp.stack(wv_p), jnp.stack(sr_p), jnp.stack(si_p),
            jnp.stack(mk_p), jnp.stack(mv_p),
            jnp.stack(wk_s), jnp.stack(wv_s), jnp.stack(sr_s), jnp.stack(si_s))
```

```python
import contextlib
import math
import numpy as np
import concourse.bass as bass
import concourse.mybir as mybir
from concourse.bass_utils import run_bass_kernel_spmd

F32 = mybir.dt.float32
BF16 = mybir.dt.bfloat16
AF = mybir.ActivationFunctionType
ALU = mybir.AluOpType

ENGS = ("pe", "act", "dve", "pool", "sp")
NCORE = 8
TP = 2048
NSLOT = 4
EPS = 1e-6
PI = math.pi


class Prog:
    def __init__(self, nc, stack, n_dma_sems=10):
        self.nc = nc
        self.q = {e: [] for e in ENGS}
        self.trace = {e: [] for e in ENGS}
        self.cnt = {e: 0 for e in ENGS}
        self.sem = {e: stack.enter_context(nc.semaphore("s_" + e)) for e in ENGS}
        self.semobj = dict(self.sem)
        self.waited = {e: {} for e in ENGS}
        self.state = {}
        self.dma_sems = {}
        for qe in ("sp", "pool", "act"):
            lst = []
            for i in range(n_dma_sems):
                k = "d_%s_%d" % (qe, i)
                self.semobj[k] = stack.enter_context(nc.semaphore(k))
                lst.append([k, 0])
            self.dma_sems[qe] = [lst, 0]

    def _need(self, eng, tok, needs):
        if tok is None:
            return
        k, v = tok
        if self.waited[eng].get(k, 0) >= v:
            return
        if needs.get(k, 0) < v:
            needs[k] = v

    def _deps(self, eng, reads, writes, is_dma):
        needs = {}
        for key in reads:
            st = self.state.get(key)
            if st is not None:
                w = st["w"]
                if w is not None and eng == "pool" and w[0] == "pool" and not is_dma:
                    continue
                self._need(eng, w, needs)
        for key in writes:
            st = self.state.get(key)
            if st is not None:
                w = st["w"]
                if w is not None and (is_dma or w[0] != eng):
                    self._need(eng, w, needs)
                for rk, tok in st["r"].items():
                    if is_dma or tok[0] != eng:
                        self._need(eng, tok, needs)
        return needs

    def _emit_waits(self, eng, needs):
        for k, v in needs.items():
            self.waited[eng][k] = v
            s = self.semobj[k]
            self.trace[eng].append(("wait", k, v))
            self.q[eng].append(lambda e, s=s, v=v: e.wait_ge(s, v))

    def _record(self, tok, rkey, reads, writes):
        for key in reads:
            st = self.state.setdefault(key, {"w": None, "r": {}})
            st["r"][rkey] = tok
        for key in writes:
            self.state[key] = {"w": tok, "r": {}}

    def op(self, eng, fn, reads=(), writes=()):
        needs = self._deps(eng, reads, writes, False)
        self._emit_waits(eng, needs)
        self.cnt[eng] += 1
        tok = (eng, self.cnt[eng])
        s = self.sem[eng]
        self.trace[eng].append(("inc", eng, 1))
        self.q[eng].append(lambda e, fn=fn, s=s: fn(e).then_inc(s, 1))
        self._record(tok, eng, reads, writes)
        return tok

    def group(self, eng, fns, reads=(), writes=()):
        needs = self._deps(eng, reads, writes, False)
        self._emit_waits(eng, needs)
        self.cnt[eng] += 1
        tok = (eng, self.cnt[eng])
        s = self.sem[eng]
        self.trace[eng].append(("inc", eng, 1))
        n = len(fns)
        for i, fn in enumerate(fns):
            if i == n - 1:
                self.q[eng].append(lambda e, fn=fn, s=s: fn(e).then_inc(s, 1))
            else:
                self.q[eng].append(lambda e, fn=fn: fn(e))
        self._record(tok, eng, reads, writes)
        return tok

    def dma(self, qe, out, in_, reads=(), writes=(), **kw):
        lst, idx = self.dma_sems[qe]
        ent = lst[idx % len(lst)]
        self.dma_sems[qe][1] = idx + 1
        k = ent[0]
        needs = self._deps(qe, reads, writes, True)
        if ent[1] > 0:
            self._need(qe, (k, ent[1]), needs)
        self._emit_waits(qe, needs)
        ent[1] += 16
        tok = (k, ent[1])
        s = self.semobj[k]
        self.trace[qe].append(("inc", k, 16))
        self.q[qe].append(lambda e, s=s, out=out, in_=in_, kw=kw: e.dma_start(out=out, in_=in_, **kw).then_inc(s, 16))
        self._record(tok, k, reads, writes)
        return tok

    def custom(self, qe, fn, semname, reads=(), writes=()):
        needs = self._deps(qe, reads, writes, True)
        self._emit_waits(qe, needs)
        s = self.semobj[semname]
        self.trace[qe].append(("inc", semname, 1))
        self.q[qe].append(lambda e, fn=fn, s=s: fn(e).then_inc(s, 1))
        self._record((semname, 1), semname, reads, writes)

    def wait_all(self, eng="sp"):
        needs = {}
        for key, st in self.state.items():
            self._need(eng, st["w"], needs)
            for tok in st["r"].values():
                self._need(eng, tok, needs)
        self._emit_waits(eng, needs)

    def replay(self, block):
        q = self.q

        @block.tensor
        def _(e):
            for f in q["pe"]:
                f(e)

        @block.scalar
        def _(e):
            for f in q["act"]:
                f(e)

        @block.vector
        def _(e):
            for f in q["dve"]:
                f(e)

        @block.gpsimd
        def _(e):
            for f in q["pool"]:
                f(e)

        @block.sync
        def _(e):
            for f in q["sp"]:
                f(e)


def apx(base, extra, dims):
    return bass.AP(base.tensor, base.offset + extra, [list(base.ap[0])] + [list(d) for d in dims])


def dram_ap(t, offset, dims):
    return bass.AP(t.tensor, t.offset + offset, [list(d) for d in dims])


IN_SPECS = [
    ("xp", [TP, 1024]), ("xh", [128, 1024]), ("xs", [128, 1024]),
    ("cwk", [16, 128, 128]), ("cwv", [16, 128, 128]),
    ("sre", [16, 32, 64]), ("sim", [16, 32, 64]),
    ("cmk", [16, 256, 1024]), ("cmv", [16, 256, 1024]), ("mem", [256, 1024]),
    ("g_ffn1", [1024]), ("w1", [1024, 5632]), ("w1o", [2816, 1024]),
    ("g_mix", [1024]), ("w_in", [1024, 3328]), ("sinks", [8]),
    ("a_re", [32, 64]), ("a_im", [32, 64]), ("log_dt", [32]),
    ("b_re", [32, 64, 16]), ("b_im", [32, 64, 16]), ("c_re", [32, 16, 64]), ("c_im", [32, 16, 64]),
    ("ssm_d", [512]), ("w_up", [512, 1024]), ("w_glu", [512, 2048]), ("w_out", [1024, 1024]),
    ("g_x", [1024]), ("g_mem", [1024]), ("w_xq", [1024, 1024]), ("w_xk", [1024, 1024]),
    ("w_xv", [1024, 1024]), ("w_xo", [1024, 1024]),
    ("g_ffn2", [1024]), ("w2", [1024, 5632]), ("w2o", [2816, 1024]), ("g_final", [1024]),
    ("ropec", [128, TP + 256]), ("ropes", [128, TP + 256]),
    ("cmask", [128, 8]), ("maskh", [128, 256]), ("maskp", [128, 256]),
    ("masksc", [128, 128]), ("maskctx", [128, 8]),
    ("identf", [128, 128]), ("permf", [128, 128]), ("jidx", [128, 128]),
]
OUT_SPECS = [
    ("yp", [TP, 1024]), ("ys", [128, 1024]),
    ("nwk_p", [128, 128]), ("nwv_p", [128, 128]), ("nsr_p", [32, 64]), ("nsi_p", [32, 64]),
    ("nmk_p", [256, 1024]), ("nmv_p", [256, 1024]),
    ("nwk_s", [16, 128, 128]), ("nwv_s", [16, 128, 128]), ("nsr_s", [16, 32, 64]), ("nsi_s", [16, 32, 64]),
]


def build_nc():
    nc = bass.Bass("TRN2", target_bir_lowering=False)
    D = {}
    for name, shape in IN_SPECS:
        D[name] = nc.dram_tensor(name, shape, F32, kind="ExternalInput").ap()
    for name, shape in OUT_SPECS:
        D[name] = nc.dram_tensor(name, shape, F32, kind="ExternalOutput").ap()
    x1d = nc.dram_tensor("x1d", [8, 128, TP], F32).ap()
    ag_in = nc.dram_tensor("ag_in", [128, 32], F32).ap()
    NWSCR = 64
    wscr = nc.dram_tensor("wscr", [NWSCR, 128, 4096], BF16).ap()
    ag_out = nc.dram_tensor("ag_out", [NCORE * 128, 32], F32).ap()

    with contextlib.ExitStack() as st:
        def sb(name, shape, dt):
            return st.enter_context(nc.sbuf_tensor("sb_" + name, shape, dt))

        identf = sb("identf", [128, 128], F32)
        identb = sb("identb", [128, 128], BF16)
        permb = sb("permb", [128, 128], BF16)
        onesb = sb("onesb", [128, 128], BF16)
        onesS = sb("onesS", [128, 128], BF16)
        maskh = sb("maskh", [128, 2, 128], BF16)
        maskp = sb("maskp", [128, 2, 128], BF16)
        masksc = sb("masksc", [128, 128], BF16)
        maskctx = sb("maskctx", [128, 8], BF16)
        mstage = sb("mstage", [128, 256], F32)
        gcol = sb("gcol", [128, 6, 8], F32)
        sinkexp = sb("sinkexp", [128, 4], F32)
        cmask = sb("cmask", [128, 8], F32)
        dcol = sb("dcol", [128, 4], F32)
        ropec = sb("ropec", [128, 512], F32)
        ropes = sb("ropes", [128, 512], F32)
        BbT = sb("BbT", [128, 16, 2, 128], BF16)
        CE = sb("CE", [128, 16, 2, 32], BF16)
        cosT = sb("cosT", [128, 16, 128], F32)
        sinT = sb("sinT", [128, 16, 128], F32)
        rmask = sb("rmask", [128, 16, 128], F32)
        sp_ = sb("sp_", [128, 40, 16], F32)
        fall = sb("fall", [128, 8, 32], F32)
        mkT = sb("mkT", [128, 8, 256], BF16)
        mvp = sb("mvp", [128, 2, 1024], BF16)
        xT = sb("xT", [128, 8, 512], F32)
        hT = sb("hT", [128, 8, 512], BF16)
        rstd = sb("rstd", [128, 512], F32)
        U1 = sb("U1", [128, 22, 512], BF16)
        U2 = sb("U2", [128, 8, 512], BF16)
        merged = sb("merged", [128, 8, 512], BF16)
        OT = sb("OT", [128, 4, 512], BF16)
        zT = sb("zT", [128, 4, 512], BF16)
        uT = sb("uT", [128, 4, 512], BF16)
        kprev = sb("kprev", [128, 128], BF16)
        vprev = sb("vprev", [128, 128], BF16)
        vtok = sb("vtok", [128, 4, 128], BF16)
        vtokf = sb("vtokf", [128, 128], F32)
        krof = sb("krof", [128, 128], F32)
        PT = sb("PT", [128, 2, 8, 128], BF16)
        rd = sb("rd", [128, 512], F32)
        tmpb = sb("tmpb", [128, 2, 512], BF16)
        SS = sb("SS", [128, 4, 1024], F32)
        hri = sb("hri", [128, 2, 1024], BF16)
        wsl = [sb("wsl%d" % i, [128, 4096], BF16) for i in range(NSLOT)]
        PSD = [st.enter_context(nc.psum_tensor("psd%d" % i, [128, 1024], F32)) for i in range(4)]

        def bank(i):
            return PSD[i // 2][:, (i % 2) * 512:(i % 2) * 512 + 512]

        def bk(i):
            return ("ps", i)

        P = Prog(nc, st)
        print('SBUF remaining', nc.sbuf_bytes_remaining, flush=True)
        st.enter_context(nc.semaphore("ccs"))
        ccs = None
        ccs = st.enter_context(nc.semaphore("ccs2"))
        P.semobj["ccs"] = ccs
        block = st.enter_context(nc.Block())

        U1k = lambda i: ("U1", i)
        U2k = lambda i: ("U2", i)
        SSv = lambda i: SS[:, i, :]

        items = []

        wcache = {}

        def flush(lookahead=NSLOT - 1):
            n = len(items)
            nl = 0
            slot_of = {}
            widx = [0]
            for i in range(n):
                while nl < n and nl <= i + lookahead:
                    loads = items[nl][0]
                    if loads:
                        ahead = sum(1 for k in range(i, nl) if items[k][0])
                        if ahead >= NSLOT - 0 and nl > i:
                            break
                        si = widx[0] % NSLOT
                        widx[0] += 1
                        slot_of[nl] = si
                        key = tuple((src.tensor.name, int(src.offset), tuple(tuple(d) for d in src.ap)) for _, src in loads)
                        if key in wcache:
                            ci_ = wcache[key]
                            P.dma("sp", wsl[si][:, :], wscr[ci_], reads=[("wscr", ci_)], writes=[("ws", si)])
                        else:
                            for dfn, src in loads:
                                P.dma("pool", dfn(wsl[si]), src, writes=[("ws", si)])
                            if len(wcache) < NWSCR:
                                ci_ = len(wcache)
                                wcache[key] = ci_
                                P.dma("sp", wscr[ci_], wsl[si][:, :], reads=[("ws", si)], writes=[("wscr", ci_)])
                    nl += 1
                loads, fn = items[i]
                if loads:
                    si = slot_of[i]
                    fn(wsl[si], ("ws", si))
                else:
                    fn(None, None)
            del items[:]

        def add(fn, loads=()):
            items.append((list(loads), fn))

        def ld_lhsT(w, nk, c0, ncols, dst_off=0, dst_kstride=None):
            ks = dst_kstride if dst_kstride is not None else ncols
            W = w.shape[1]
            src = dram_ap(w, c0, [[W, 128], [128 * W, nk], [1, ncols]])
            return (lambda slot, ks=ks, nk=nk, ncols=ncols, dst_off=dst_off: apx(slot[:, :], dst_off, [[ks, nk], [1, ncols]]), src)

        rot = {"mm": 0}

        def next_bank(lo=0, hi=4):
            b = lo + rot["mm"] % (hi - lo)
            rot["mm"] += 1
            return b

        def norm(T, gi, out_f32=None):
            def f(_s, _k):
                for kc in range(8):
                    P.op("act", lambda e, kc=kc: e.activation(out=U2[:, kc, :T], in_=xT[:, kc, :T], func=AF.Square),
                         reads=[("xT", kc)], writes=[U2k(kc)])
                P.group("pe", [lambda e, kc=kc: e.matmul(bank(4)[:, :T], lhsT=onesS[:, :], rhs=U2[:, kc, :T],
                                                         start=(kc == 0), stop=(kc == 7)) for kc in range(8)],
                        reads=[U2k(kc) for kc in range(8)] + ["onesS"], writes=[bk(4)])
                P.op("dve", lambda e: e.tensor_single_scalar(out=rstd[:, :T], in_=bank(4)[:, :T], scalar=EPS, op=ALU.add),
                     reads=[bk(4)], writes=["rstd"])
                P.op("act", lambda e: e.activation(out=rstd[:, :T], in_=rstd[:, :T], func=AF.Sqrt), reads=["rstd"], writes=["rstd"])
                P.op("dve", lambda e: e.reciprocal(out=rstd[:, :T], in_=rstd[:, :T]), reads=["rstd"], writes=["rstd"])
                for kc in range(8):
                    eng = "dve"
                    if out_f32 is None:
                        P.op(eng, lambda e, kc=kc: e.scalar_tensor_tensor(out=hT[:, kc, :T], in0=xT[:, kc, :T],
                                                                         scalar=gcol[:, gi, kc:kc + 1], in1=rstd[:, :T],
                                                                         op0=ALU.mult, op1=ALU.mult),
                             reads=[("xT", kc), "rstd", "gcol"], writes=[("hT", kc)])
                    else:
                        P.op(eng, lambda e, kc=kc: e.scalar_tensor_tensor(out=xT[:, kc, :T], in0=xT[:, kc, :T],
                                                                         scalar=gcol[:, gi, kc:kc + 1], in1=rstd[:, :T],
                                                                         op0=ALU.mult, op1=ALU.mult),
                             reads=[("xT", kc), "rstd", "gcol"], writes=[("xT", kc)])
            add(f)

        def proj_fm(T, w, c0, nch, src, srck, nk, evac):
            done = 0
            while done < nch:
                n = min(4096 // (nk * 128), nch - done)
                ncols = n * 128

                def f(slot, sk, done=done, n=n, ncols=ncols):
                    for i in range(n):
                        b = next_bank()
                        P.group("pe", [lambda e, kc=kc, i=i, b=b: e.matmul(bank(b)[:, :T],
                                                                          lhsT=slot[:, kc * ncols + i * 128: kc * ncols + i * 128 + 128],
                                                                          rhs=src(kc)[:, :T], start=(kc == 0), stop=(kc == nk - 1))
                                       for kc in range(nk)],
                                reads=[sk] + [srck(kc) for kc in range(nk)], writes=[bk(b)])
                        evac(done + i, bank(b)[:, :T], bk(b))
                add(f, [ld_lhsT(w, nk, c0 + done * 128, ncols)])
                done += n

        def ffn(T, w_i, w_o):
            for fp in range(11):
                def f(slot, sk, fp=fp):
                    for fo in range(2):
                        fch = 2 * fp + fo
                        ba = next_bank()
                        bb = next_bank()
                        for (b, off) in ((ba, 0), (bb, 2048)):
                            P.group("pe", [lambda e, kc=kc, b=b, off=off, fo=fo: e.matmul(
                                bank(b)[:, :T], lhsT=slot[:, off + kc * 256 + fo * 128: off + kc * 256 + fo * 128 + 128],
                                rhs=hT[:, kc, :T], start=(kc == 0), stop=(kc == 7)) for kc in range(8)],
                                reads=[sk] + [("hT", kc) for kc in range(8)], writes=[bk(b)])
                        P.op("act", lambda e, ba=ba, fo=fo: e.activation(out=tmpb[:, fo, :T], in_=bank(ba)[:, :T], func=AF.Silu),
                             reads=[bk(ba)], writes=[("tmpb", fo)])
                        P.op("dve", lambda e, bb=bb, fch=fch, fo=fo: e.tensor_tensor(out=U1[:, fch, :T], in0=bank(bb)[:, :T],
                                                                                   in1=tmpb[:, fo, :T], op=ALU.mult),
                             reads=[bk(bb), ("tmpb", fo)], writes=[U1k(fch)])
                add(f, [ld_lhsT(w_i, 8, fp * 256, 256, 0), ld_lhsT(w_i, 8, 2816 + fp * 256, 256, 2048)])
            for dc in range(8):
                def f(slot, sk, dc=dc):
                    b = next_bank()
                    P.group("pe", [lambda e, fc=fc, b=b: e.matmul(bank(b)[:, :T], lhsT=slot[:, fc * 128:(fc + 1) * 128],
                                                                rhs=U1[:, fc, :T], start=(fc == 0), stop=(fc == 21))
                                   for fc in range(22)],
                            reads=[sk] + [U1k(fc) for fc in range(22)], writes=[bk(b)])
                    P.op("dve", lambda e, b=b, dc=dc: e.scalar_tensor_tensor(out=xT[:, dc, :T], in0=bank(b)[:, :T], scalar=0.5,
                                                                           in1=xT[:, dc, :T], op0=ALU.mult, op1=ALU.add),
                         reads=[bk(b), ("xT", dc)], writes=[("xT", dc)])
                add(f, [ld_lhsT(w_o, 22, dc * 128, 128)])

        def resid_proj(T, w, src, srck):
            def evac(i, ps, pk):
                P.op("dve", lambda e, i=i, ps=ps: e.tensor_tensor(out=xT[:, i, :T], in0=ps, in1=xT[:, i, :T], op=ALU.add),
                     reads=[pk, ("xT", i)], writes=[("xT", i)])
            proj_fm(T, w, 0, 8, src, srck, 8, evac)

        def load_x(T, src_rows):
            nb = T // 128

            def f(_s, _k):
                for blk in range(nb):
                    P.dma("sp", SS[:, blk, :], src_rows[blk * 128:(blk + 1) * 128, :], writes=[("SS", blk)])
                for blk in range(nb):
                    for h in range(2):
                        b = 6 + (blk * 2 + h) % 2
                        P.group("pe", [lambda e, i=i, b=b, blk=blk, h=h: e.transpose(bank(b)[:, i * 128:(i + 1) * 128],
                                                                                 SS[:, blk, (h * 4 + i) * 128:(h * 4 + i + 1) * 128], identf[:, :])
                                       for i in range(4)], reads=[("SS", blk), "identf"], writes=[bk(b)])
                        eng = "dve" if h == 0 else "act"
                        if eng == "dve":
                            P.op("dve", lambda e, b=b, blk=blk, h=h: e.tensor_copy(
                                out=xT[:, h * 4:h * 4 + 4, blk * 128:(blk + 1) * 128],
                                in_=bank(b).rearrange("p (i t) -> p i t", i=4)),
                                reads=[bk(b)], writes=[("xT", h * 4 + i) for i in range(4)])
                        else:
                            P.op("act", lambda e, b=b, blk=blk, h=h: e.copy(
                                out=xT[:, h * 4:h * 4 + 4, blk * 128:(blk + 1) * 128],
                                in_=bank(b).rearrange("p (i t) -> p i t", i=4)),
                                reads=[bk(b)], writes=[("xT", h * 4 + i) for i in range(4)])
            add(f)

        def store_y(T, dst_rows):
            nb = T // 128
            norm(T, 5, out_f32=True)

            def f(_s, _k):
                for blk in range(nb):
                    for h in range(2):
                        b = 6 + (blk * 2 + h) % 2
                        P.group("pe", [lambda e, i=i, b=b, blk=blk, h=h: e.transpose(bank(b)[:, i * 128:(i + 1) * 128],
                                                                                 xT[:, h * 4 + i, blk * 128:(blk + 1) * 128], identf[:, :])
                                       for i in range(4)], reads=[("xT", h * 4 + i) for i in range(4)] + ["identf"], writes=[bk(b)])
                        if h == 0:
                            P.op("dve", lambda e, b=b, blk=blk, h=h: e.tensor_copy(out=SS[:, blk, h * 512:(h + 1) * 512], in_=bank(b)),
                                 reads=[bk(b)], writes=[("SS", blk)])
                        else:
                            P.op("act", lambda e, b=b, blk=blk, h=h: e.copy(out=SS[:, blk, h * 512:(h + 1) * 512], in_=bank(b)),
                                 reads=[bk(b)], writes=[("SS", blk)])
                    P.dma("sp", dst_rows[blk * 128:(blk + 1) * 128, :], SS[:, blk, :], reads=[("SS", blk)])
            add(f)

        def setup_consts(_s, _k):
            P.dma("sp", identf[:, :], D["identf"], writes=["identf"])
            P.op("dve", lambda e: e.tensor_copy(out=identb[:, :], in_=identf[:, :]), reads=["identf"], writes=["identb"])
            P.dma("sp", mstage[:, 0:128], D["permf"], writes=["mstage"])
            P.op("dve", lambda e: e.tensor_copy(out=permb[:, :], in_=mstage[:, 0:128]), reads=["mstage"], writes=["permb"])
            P.op("dve", lambda e: e.memset(onesb[:, :], 1.0), writes=["onesb"])
            P.op("dve", lambda e: e.memset(onesS[:, :], 1.0 / 1024.0), writes=["onesS"])
            for nm, t in (("maskh", maskh), ("maskp", maskp)):
                P.dma("sp", mstage[:, :], D[nm], writes=["mstage"])
                P.op("dve", lambda e, t=t: e.tensor_copy(out=t[:, :, :], in_=mstage[:, :].rearrange("p (a b) -> p a b", a=2)),
                     reads=["mstage"], writes=[nm])
            P.dma("sp", mstage[:, 0:128], D["masksc"], writes=["mstage"])
            P.op("dve", lambda e: e.tensor_copy(out=masksc[:, :], in_=mstage[:, 0:128]), reads=["mstage"], writes=["masksc"])
            P.dma("sp", mstage[:, 0:8], D["maskctx"], writes=["mstage"])
            P.op("dve", lambda e: e.tensor_copy(out=maskctx[:, :], in_=mstage[:, 0:8]), reads=["mstage"], writes=["maskctx"])
            P.dma("sp", cmask[:, :], D["cmask"], writes=["cmask"])
            for gi, nm in enumerate(["g_ffn1", "g_mix", "g_x", "g_ffn2", "g_mem", "g_final"]):
                P.dma("sp", gcol[:, gi, :], dram_ap(D[nm], 0, [[1, 128], [128, 8]]), writes=["gcol"],
                      allow_slow_non_contiguous=True)
            P.dma("sp", dcol[:, :], dram_ap(D["ssm_d"], 0, [[1, 128], [128, 4]]), writes=["dcol"], allow_slow_non_contiguous=True)
            for h in range(2):
                P.dma("sp", sinkexp[64 * h:64 * h + 64, :], dram_ap(D["sinks"], 4 * h, [[0, 64], [1, 4]]), writes=["sinkexp"])
            P.op("act", lambda e: e.activation(out=sinkexp[:, :], in_=sinkexp[:, :], func=AF.Exp), reads=["sinkexp"], writes=["sinkexp"])
        add(setup_consts)

        V = lambda i: sp_[:, i, :]
        (A_RE, A_IM, DT, ARD, TH, MAG, LRE, LIM, DEN, NR, KRE, KIM, T0, T1, T2, T3,
         W128R, W128I, C127, S127, L2KR, L2KI, HRE, HIM, INJR, INJI, ACCR, ACCI, T4, T5, G127R, G127I) = range(32)
        SPK = "sp_"

        def vop(eng, fn, extra_r=(), extra_w=()):
            P.op(eng, fn, reads=[SPK] + list(extra_r), writes=[SPK] + list(extra_w))

        def tt(o, a, b, op, eng="dve"):
            vop(eng, lambda e: e.tensor_tensor(out=V(o), in0=V(a), in1=V(b), op=op))

        I32 = mybir.dt.int32
        TWO_PI = 2 * PI

        def rr_sin(out, a, shift, it, t1, t2, rk, wk, out_wk):
            R = list(rk) + list(wk)
            P.op("dve", lambda e: e.tensor_scalar(out=t1, in0=a, scalar1=shift, scalar2=1.0 / TWO_PI, op0=ALU.add, op1=ALU.mult), reads=R, writes=wk)
            P.op("dve", lambda e: e.tensor_copy(out=it, in_=t1), reads=R, writes=wk)
            P.op("dve", lambda e: e.tensor_copy(out=t1, in_=it), reads=R, writes=wk)
            P.op("dve", lambda e: e.tensor_scalar(out=t1, in0=t1, scalar1=-TWO_PI, scalar2=shift, op0=ALU.mult, op1=ALU.add), reads=R, writes=wk)
            P.op("dve", lambda e: e.tensor_tensor(out=t1, in0=t1, in1=a, op=ALU.add), reads=R, writes=wk)
            P.op("dve", lambda e: e.tensor_scalar(out=t2, in0=t1, scalar1=PI, scalar2=-TWO_PI, op0=ALU.is_gt, op1=ALU.mult), reads=R, writes=wk)
            P.op("dve", lambda e: e.tensor_tensor(out=t1, in0=t1, in1=t2, op=ALU.add), reads=R, writes=wk)
            P.op("dve", lambda e: e.tensor_scalar(out=t2, in0=t1, scalar1=-PI, scalar2=TWO_PI, op0=ALU.is_lt, op1=ALU.mult), reads=R, writes=wk)
            P.op("dve", lambda e: e.tensor_tensor(out=t1, in0=t1, in1=t2, op=ALU.add), reads=R, writes=wk)
            P.op("dve", lambda e: e.tensor_scalar(out=t1, in0=t1, scalar1=3.1415925, scalar2=-3.1415925, op0=ALU.min, op1=ALU.max), reads=R, writes=wk)
            P.op("act", lambda e: e.activation(out=out, in_=t1, func=AF.Sin), reads=R, writes=list(out_wk))

        def sincos(o_sin, o_cos, ang_idx, shape_fn=None):
            it = rstd[:, 0:16].bitcast(I32)
            for o, shift in ((o_sin, 0.0), (o_cos, PI / 2)):
                rr_sin(V(o), V(ang_idx), shift, it, V(T4), V(T2), [SPK], [SPK, "rstd"], [SPK])

        def cmul(or_, oi_, ar, ai, br, bi):
            tt(T0, ar, br, ALU.mult)
            tt(T1, ai, bi, ALU.mult)
            tt(T2, ar, bi, ALU.mult)
            tt(T3, ai, br, ALU.mult)
            tt(or_, T0, T1, ALU.subtract)
            tt(oi_, T2, T3, ALU.add)

        def setup_ssm(_s, _k):
            for two in range(2):
                for nm, idx in (("a_re", A_RE), ("a_im", A_IM)):
                    P.dma("sp", sp_[64 * two:64 * two + 64, idx, :], dram_ap(D[nm], two * 64, [[1, 64], [128, 16]]),
                          writes=[SPK], allow_slow_non_contiguous=True)
                P.dma("sp", sp_[64 * two:64 * two + 64, DT, :], dram_ap(D["log_dt"], two, [[0, 64], [2, 16]]),
                      writes=[SPK], allow_slow_non_contiguous=True)
            vop("act", lambda e: e.activation(out=V(DT), in_=V(DT), func=AF.Exp))
            tt(ARD, A_RE, DT, ALU.mult)
            tt(TH, A_IM, DT, ALU.mult)
            vop("act", lambda e: e.activation(out=V(MAG), in_=V(ARD), func=AF.Exp))
            sincos(T5, T0 + 0, TH)
            tt(LIM, MAG, T5, ALU.mult)
            tt(LRE, MAG, T0, ALU.mult)
            tt(T0, A_RE, A_RE, ALU.mult)
            tt(T1, A_IM, A_IM, ALU.mult)
            tt(DEN, T0, T1, ALU.add)
            vop("dve", lambda e: e.reciprocal(out=V(DEN), in_=V(DEN)))
            vop("dve", lambda e: e.tensor_single_scalar(out=V(NR), in_=V(LRE), scalar=-1.0, op=ALU.add))
            tt(T0, NR, A_RE, ALU.mult)
            tt(T1, LIM, A_IM, ALU.mult)
            tt(T2, T0, T1, ALU.add)
            tt(KRE, T2, DEN, ALU.mult)
            tt(T0, LIM, A_RE, ALU.mult)
            tt(T1, NR, A_IM, ALU.mult)
            tt(T2, T0, T1, ALU.subtract)
            tt(KIM, T2, DEN, ALU.mult)
            P.dma("sp", SS[:, 0, 0:128], D["jidx"], writes=[("SS", 0)])
            ang = SS[:, 1, :].rearrange("p (a b) -> p a b", a=8)
            red = SS[:, 2, :].rearrange("p (a b) -> p a b", a=8)
            for half in range(2):
                prs = slice(8 * half, 8 * half + 8)
                P.op("dve", lambda e, prs=prs: e.tensor_tensor(out=ang, in0=apx(sp_[:, TH, prs], 0, [[1, 8], [0, 128]]),
                                                             in1=apx(SS[:, 0, 0:128], 0, [[0, 8], [1, 128]]), op=ALU.mult),
                     reads=[SPK, ("SS", 0)], writes=[("SS", 1)])
                it3 = SS[:, 3, :].bitcast(I32).rearrange("p (a b) -> p a b", a=8)
                tmp3 = SS[:, 0, :].rearrange("p (a b) -> p a b", a=8) if False else None
                for tab, shift in ((sinT, 0.0), (cosT, PI / 2)):
                    rr_sin(tab[:, prs, :], ang, shift, it3, red, apx(hri[:, :, :].rearrange("p a b -> p (a b)").bitcast(F32), 0, [[128, 8], [1, 128]]),
                           [("SS", 1)], [("SS", 2), ("SS", 3), ("hri", 0), ("hri", 1)], ["rot"])
            P.op("dve", lambda e: e.tensor_copy(out=rmask[:, :, :], in_=apx(sp_[:, MAG, :], 0, [[1, 16], [0, 128]])),
                 reads=[SPK], writes=["rmask"])
            P.op("dve", lambda e: e.memset(rmask[:, :, 0:1], 0.0), writes=["rmask"])
            P.op("dve", lambda e: e.tensor_copy(out=V(C127), in_=cosT[:, :, 127]), reads=["rot"], writes=[SPK])
            P.op("dve", lambda e: e.tensor_copy(out=V(S127), in_=sinT[:, :, 127]), reads=["rot"], writes=[SPK])
            cmul(W128R, W128I, LRE, LIM, C127, S127)
            vop("dve", lambda e: e.tensor_single_scalar(out=V(T5), in_=V(TH), scalar=2048.0, op=ALU.mult))
            sincos(L2KI, L2KR, T5)
            vop("act", lambda e: e.activation(out=V(T5), in_=V(ARD), func=AF.Exp, scale=2048.0))
            tt(L2KR, L2KR, T5, ALU.mult)
            tt(L2KI, L2KI, T5, ALU.mult)
            bn = SS[:, 1, :].rearrange("p (r x) -> p r x", r=16)[:, :, 0:64]
            P.op("dve", lambda e: e.memset(SS[:, 1, :], 0.0), writes=[("SS", 1)])
            for two in range(2):
                for ci, nm in enumerate(("b_re", "b_im")):
                    P.dma("sp", apx(SS[64 * two:64 * two + 64, 1, :], ci * 32 + two * 16, [[64, 16], [1, 16]]),
                          dram_ap(D[nm], two * 1024, [[16, 64], [2048, 16], [1, 16]]), writes=[("SS", 1)])
            bre = apx(SS[:, 1, :], 0, [[64, 16], [1, 32]])
            bim = apx(SS[:, 1, :], 32, [[64, 16], [1, 32]])
            kre_b = apx(sp_[:, KRE, :], 0, [[1, 16], [0, 32]])
            kim_b = apx(sp_[:, KIM, :], 0, [[1, 16], [0, 32]])
            s2 = SS[:, 2, :]
            t_a = apx(s2, 0, [[32, 16], [1, 32]])
            t_b = apx(s2, 512, [[32, 16], [1, 32]])
            bbr = apx(SS[:, 3, :], 0, [[32, 16], [1, 32]])
            bbi = apx(SS[:, 3, :], 512, [[32, 16], [1, 32]])
            bbb = hri[:, 0, :]
            P.op("dve", lambda e: e.tensor_tensor(out=t_a, in0=bre, in1=kre_b, op=ALU.mult), reads=[("SS", 1), SPK], writes=[("SS", 2)])
            P.op("dve", lambda e: e.tensor_tensor(out=t_b, in0=bim, in1=kim_b, op=ALU.mult), reads=[("SS", 1), SPK], writes=[("SS", 2)])
            P.op("dve", lambda e: e.tensor_tensor(out=bbr, in0=t_a, in1=t_b, op=ALU.subtract), reads=[("SS", 2)], writes=[("SS", 3)])
            P.op("dve", lambda e: e.tensor_tensor(out=t_a, in0=bre, in1=kim_b, op=ALU.mult), reads=[("SS", 1), SPK, ("SS", 3)], writes=[("SS", 2)])
            P.op("dve", lambda e: e.tensor_tensor(out=t_b, in0=bim, in1=kre_b, op=ALU.mult), reads=[("SS", 1), SPK], writes=[("SS", 2)])
            P.op("dve", lambda e: e.tensor_tensor(out=bbi, in0=t_a, in1=t_b, op=ALU.add), reads=[("SS", 2)], writes=[("SS", 3)])
            P.op("dve", lambda e: e.tensor_copy(out=bbb[:, 0:1024], in_=SS[:, 3, :]), reads=[("SS", 3)], writes=["hri"])
            P.op("dve", lambda e: e.memset(BbT[:, :, :, :], 0.0), writes=["BbT"])
            for ct in range(4):
                fns = []
                for q in range(4):
                    pr = ct * 4 + q
                    for ci in range(2):
                        fns.append(lambda e, q=q, pr=pr, ci=ci: e.matmul(
                            bank(0)[32 * q:32 * q + 32, ci * 128:(ci + 1) * 128],
                            lhsT=bbb[:, ci * 512 + pr * 32: ci * 512 + pr * 32 + 32], rhs=identb[:, :],
                            start=True, stop=True, tile_position=(0, 32 * q)))
                P.group("pe", fns, reads=["hri", "identb"], writes=[bk(0)])
                for q in range(4):
                    P.op("dve", lambda e, ct=ct, q=q: e.tensor_copy(out=BbT[32 * q:32 * q + 32, ct * 4 + q, :, :],
                                                                  in_=bank(0)[32 * q:32 * q + 32, 0:256].rearrange("p (c n) -> p c n", c=2)),
                         reads=[bk(0)], writes=["BbT"])
            for ci, nm in enumerate(("c_re", "c_im")):
                P.op("dve", lambda e: e.memset(SS[0:32, 1, :], 0.0), reads=[("SS", 1)], writes=[("SS", 1)])
                P.op("dve", lambda e: e.memset(SS[0:32, 2, :], 0.0), reads=[("SS", 2)], writes=[("SS", 2)])
                for two in range(2):
                    for hf in range(2):
                        P.dma("sp", apx(SS[16 * two:16 * two + 16, 1 + hf, :], two * 64, [[128, 8], [1, 64]]),
                              dram_ap(D[nm], two * 1024 + hf * 8 * 2048, [[64, 16], [2048, 8], [1, 64]]), writes=[("SS", 1 + hf)])
                fns = []
                for pr in range(16):
                    hf, pl = pr // 8, pr % 8
                    fns.append(lambda e, pr=pr, hf=hf, pl=pl: e.transpose(bank(0)[:, pr * 32:(pr + 1) * 32],
                                                                           SS[0:32, 1 + hf, pl * 128:(pl + 1) * 128], identf[0:32, 0:32]))
                P.group("pe", fns, reads=[("SS", 1), ("SS", 2), "identf"], writes=[bk(0)])
                if ci == 0:
                    P.op("dve", lambda e: e.tensor_copy(out=CE[:, :, 0, :], in_=bank(0).rearrange("p (r c) -> p r c", r=16)),
                         reads=[bk(0)], writes=["CE"])
                else:
                    P.op("dve", lambda e: e.tensor_single_scalar(out=CE[:, :, 1, :], in_=bank(0).rearrange("p (r c) -> p r c", r=16),
                                                                 scalar=-1.0, op=ALU.mult), reads=[bk(0)], writes=["CE"])
        add(setup_ssm)

        def setup_mem():
            def f(_s, _k):
                for kb in range(2):
                    P.dma("sp", SS[:, kb, :], D["mem"][kb * 128:(kb + 1) * 128, :], writes=[("SS", kb)])
                for kb in range(2):
                    P.op("act", lambda e, kb=kb: e.activation(out=SS[:, 2, :], in_=SS[:, kb, :], func=AF.Square, accum_out=rd[:, kb:kb + 1]),
                         reads=[("SS", kb)], writes=[("SS", 2), "rd"])
                P.op("dve", lambda e: e.tensor_scalar(out=rd[:, 0:2], in0=rd[:, 0:2], scalar1=1.0 / 1024.0, scalar2=EPS,
                                                      op0=ALU.mult, op1=ALU.add), reads=["rd"], writes=["rd"])
                P.op("act", lambda e: e.activation(out=rd[:, 0:2], in_=rd[:, 0:2], func=AF.Sqrt), reads=["rd"], writes=["rd"])
                P.op("dve", lambda e: e.reciprocal(out=rd[:, 0:2], in_=rd[:, 0:2]), reads=["rd"], writes=["rd"])
                for kb in range(2):
                    P.op("dve", lambda e, kb=kb: e.tensor_single_scalar(out=SS[:, kb, :], in_=SS[:, kb, :], scalar=rd[:, kb:kb + 1],
                                                                      op=ALU.mult), reads=[("SS", kb), "rd"], writes=[("SS", kb)])
                    for h in range(2):
                        b = 6 + h
                        P.group("pe", [lambda e, i=i, b=b, kb=kb, h=h: e.transpose(bank(b)[:, i * 128:(i + 1) * 128],
                                                                               SS[:, kb, (h * 4 + i) * 128:(h * 4 + i + 1) * 128], identf[:, :])
                                       for i in range(4)], reads=[("SS", kb), "identf"], writes=[bk(b)])
                        for i in range(4):
                            kc = h * 4 + i
                            P.op("dve", lambda e, b=b, i=i, kc=kc, kb=kb: e.tensor_single_scalar(
                                out=hT[:, kc, kb * 128:(kb + 1) * 128], in_=bank(b)[:, i * 128:(i + 1) * 128],
                                scalar=gcol[:, 4, kc:kc + 1], op=ALU.mult),
                                reads=[bk(b), "gcol"], writes=[("hT", kc)])
            add(f)
            for wi, (wn, on) in enumerate((("w_xk", "nmk_p"), ("w_xv", "nmv_p"))):
                for half in range(2):
                    def f(slot, sk, wi=wi, on=on, half=half):
                        for kb in range(2):
                            b = next_bank()
                            P.group("pe", [lambda e, kc=kc, b=b, kb=kb: e.matmul(bank(b), lhsT=hT[:, kc, kb * 128:(kb + 1) * 128],
                                                                               rhs=slot[:, kc * 512:(kc + 1) * 512], start=(kc == 0), stop=(kc == 7))
                                           for kc in range(8)], reads=[sk] + [("hT", kc) for kc in range(8)], writes=[bk(b)])
                            P.op("act", lambda e, b=b, kb=kb: e.copy(out=SS[:, 2 + kb, 0:512], in_=bank(b)), reads=[bk(b)], writes=[("SS", 2 + kb)])
                            P.dma("sp", D[on][kb * 128:(kb + 1) * 128, half * 512:(half + 1) * 512], SS[:, 2 + kb, 0:512], reads=[("SS", 2 + kb)])
                            if wi == 1:
                                P.op("dve", lambda e, kb=kb, half=half: e.tensor_copy(out=mvp[:, kb, half * 512:(half + 1) * 512], in_=SS[:, 2 + kb, 0:512]),
                                     reads=[("SS", 2 + kb)], writes=["mvp"])
                        if wi == 0:
                            for i in range(4):
                                b = next_bank()
                                P.group("pe", [lambda e, kc=kc, b=b, i=i: e.matmul(bank(b)[:, 0:256], lhsT=slot[:, kc * 512 + i * 128: kc * 512 + i * 128 + 128],
                                                                                 rhs=hT[:, kc, 0:256], start=(kc == 0), stop=(kc == 7)) for kc in range(8)],
                                        reads=[sk] + [("hT", kc) for kc in range(8)], writes=[bk(b)])
                                P.op("act", lambda e, b=b, i=i, half=half: e.copy(out=mkT[:, half * 4 + i, :], in_=bank(b)[:, 0:256]),
                                     reads=[bk(b)], writes=["mkT"])
                    add(f, [ld_lhsT(D[wn], 8, half * 512, 512)])
        setup_mem()

        def load_rope(T, col0):
            def f(_s, _k):
                P.dma("sp", ropec[:, :T], D["ropec"][:, col0:col0 + T], writes=["ropec"])
                P.dma("sp", ropes[:, :T], D["ropes"][:, col0:col0 + T], writes=["ropes"])
            add(f)

        def rope_chunk(T, raw, rawk, out, outk, f32out=None):
            b = next_bank()
            P.group("pe", [lambda e: e.matmul(bank(b)[:, :T], lhsT=permb[:, :], rhs=raw, start=True, stop=True)],
                    reads=[rawk, "permb"], writes=[bk(b)])
            P.op("dve", lambda e: e.tensor_tensor(out=SS[:, 3, :T], in0=bank(b)[:, :T], in1=ropes[:, :T], op=ALU.mult),
                 reads=[bk(b), "ropes"], writes=[("SS", 3)])
            P.op("pool", lambda e: e.tensor_tensor(out=SS[:, 3, 512:512 + T], in0=raw, in1=ropec[:, :T], op=ALU.mult),
                 reads=[rawk, "ropec"], writes=[("SS", 3)])
            P.op("dve", lambda e: e.tensor_tensor(out=out, in0=SS[:, 3, :T], in1=SS[:, 3, 512:512 + T], op=ALU.add),
                 reads=[("SS", 3)], writes=[outk])
            if f32out is not None:
                t0 = T - 128
                P.op("pool", lambda e: e.tensor_tensor(out=f32out, in0=SS[:, 3, t0:T], in1=SS[:, 3, 512 + t0:512 + T], op=ALU.add),
                     reads=[("SS", 3)], writes=["krof"])

        def inproj_q(T):
            W = 3328

            def lq(s, j):
                src = dram_ap(D["w_in"], s * 256 + j * 64, [[W, 128], [128 * W, 8], [1, 64]])
                return (lambda slot, s=s, j=j: apx(slot[:, :], j * 128 + s * 64, [[512, 8], [1, 64]]), src)

            def f(slot, sk):
                for j in range(4):
                    b = next_bank()
                    P.group("pe", [lambda e, kc=kc, j=j, b=b: e.matmul(bank(b)[:, :T], lhsT=slot[:, kc * 512 + j * 128: kc * 512 + j * 128 + 128],
                                                                     rhs=hT[:, kc, :T], start=(kc == 0), stop=(kc == 7)) for kc in range(8)],
                            reads=[sk] + [("hT", kc) for kc in range(8)], writes=[bk(b)])
                    P.op("act", lambda e, j=j, b=b: e.copy(out=U2[:, j, :T], in_=bank(b)[:, :T]), reads=[bk(b)], writes=[U2k(j)])
                    rope_chunk(T, U2[:, j, :T], U2k(j), U1[:, 16 + j, :T], U1k(16 + j))
            add(f, [lq(s_, j_) for s_ in range(2) for j_ in range(4)])

        def inproj_kv(T, want_f32):
            nb = T // 128

            def f(slot, sk):
                b = next_bank()
                P.group("pe", [lambda e, kc=kc, b=b: e.matmul(bank(b)[:, :T], lhsT=slot[:, kc * 256: kc * 256 + 128],
                                                            rhs=hT[:, kc, :T], start=(kc == 0), stop=(kc == 7)) for kc in range(8)],
                        reads=[sk] + [("hT", kc) for kc in range(8)], writes=[bk(b)])
                P.op("act", lambda e, b=b: e.copy(out=U2[:, 4, :T], in_=bank(b)[:, :T]), reads=[bk(b)], writes=[U2k(4)])
                rope_chunk(T, U2[:, 4, :T], U2k(4), U1[:, 20, :T], U1k(20), f32out=(krof[:, :] if want_f32 else None))
                b = next_bank()
                for blk in range(nb):
                    P.group("pe", [lambda e, kc=kc, b=b, blk=blk: e.matmul(bank(b)[:, blk * 128:(blk + 1) * 128],
                                                                          lhsT=hT[:, kc, blk * 128:(blk + 1) * 128],
                                                                          rhs=slot[:, kc * 256 + 128: kc * 256 + 256],
                                                                          start=(kc == 0), stop=(kc == 7)) for kc in range(8)],
                            reads=[sk] + [("hT", kc) for kc in range(8)], writes=[bk(b)])
                P.op("act", lambda e, b=b: e.copy(out=vtok[:, 0:nb, :], in_=bank(b)[:, :T].rearrange("p (a c) -> p a c", a=nb)),
                     reads=[bk(b)], writes=["vtok"])
                if want_f32:
                    P.op("dve", lambda e, b=b: e.tensor_copy(out=vtokf[:, :], in_=bank(b)[:, T - 128:T]), reads=[bk(b)], writes=["vtokf"])
            add(f, [ld_lhsT(D["w_in"], 8, 512, 256)])

        def inproj_u(T):
            def evac(i, ps, pk):
                P.op("act", lambda e, i=i, ps=ps: e.copy(out=uT[:, i, :T], in_=ps), reads=[pk], writes=[("uT", i)])
            proj_fm(T, D["w_in"], 768, 4, lambda kc: hT[:, kc, :], lambda kc: ("hT", kc), 8, evac)

        def inproj_gates(T):
            def evac_a(i, ps, pk):
                P.op("act", lambda e, i=i, ps=ps: e.activation(out=U1[:, i, :T], in_=ps, func=AF.Sigmoid), reads=[pk], writes=[U1k(i)])

            def evac_b(i, ps, pk):
                P.op("act", lambda e, i=i, ps=ps: e.activation(out=U1[:, 8 + i, :T], in_=ps, func=AF.Sigmoid), reads=[pk], writes=[U1k(8 + i)])
            proj_fm(T, D["w_in"], 1280, 8, lambda kc: hT[:, kc, :], lambda kc: ("hT", kc), 8, evac_a)
            proj_fm(T, D["w_in"], 2304, 8, lambda kc: hT[:, kc, :], lambda kc: ("hT", kc), 8, evac_b)

        qT = lambda: U1[:, 16:20, :]
        kT = lambda: U1[:, 20, :]
        qTk = [U1k(16 + j) for j in range(4)]

        def attn_finish(T, t0, nq):
            for kvh in range(2):
                rows = slice(64 * kvh, 64 * kvh + 64)
                dn = bank(6 + kvh)[rows, 0:4 * nq].rearrange("p (j q) -> p j q", j=4)
                ov = bank(4 + kvh)[rows, 0:4 * nq].rearrange("p (j q) -> p j q", j=4)
                rdv = rd[rows, 0:4 * nq].rearrange("p (j q) -> p j q", j=4)
                P.op("dve", lambda e, dn=dn, rdv=rdv, rows=rows: e.tensor_tensor(out=rdv, in0=dn, in1=apx(sinkexp[rows, :], 0, [[1, 4], [0, nq]]), op=ALU.add),
                     reads=[bk(6 + kvh), "sinkexp"], writes=[("rd", kvh)])
                P.op("dve", lambda e, rdv=rdv: e.reciprocal(out=rdv, in_=rdv), reads=[("rd", kvh)], writes=[("rd", kvh)])
                P.op("dve", lambda e, ov=ov, rdv=rdv, rows=rows: e.tensor_tensor(out=OT[rows, :, t0:t0 + nq], in0=ov, in1=rdv, op=ALU.mult),
                     reads=[bk(4 + kvh), ("rd", kvh)], writes=[("OT", kvh)])

        def attn_prompt(T, first_mask):
            nb = T // 128

            def f(_s, _k):
                for blk in range(nb):
                    msk = first_mask if blk == 0 else maskp
                    mk_ = "maskh" if (blk == 0 and first_mask is maskh) else "maskp"
                    tq = slice(blk * 128, (blk + 1) * 128)
                    if blk == 0:
                        kp, kpk, vp, vpk = kprev[:, :], "kprev", vprev[:, :], "vprev"
                    else:
                        kp, kpk, vp, vpk = U1[:, 20, (blk - 1) * 128:blk * 128], U1k(20), vtok[:, blk - 1, :], "vtok"
                    for ci, (kk, kkk) in enumerate(((kp, kpk), (U1[:, 20, tq], U1k(20)))):
                        fns = []
                        for kvh in range(2):
                            rows = slice(64 * kvh, 64 * kvh + 64)
                            fns.append(lambda e, kvh=kvh, rows=rows, kk=kk, ci=ci, tq=tq: e.matmul(
                                PSD[ci][:, kvh * 512:(kvh + 1) * 512], lhsT=kk[rows, :], rhs=U1[rows, 16:20, tq],
                                start=True, stop=True, tile_position=(64 * kvh, 0)))
                        P.group("pe", fns, reads=[kkk] + qTk, writes=[bk(2 * ci), bk(2 * ci + 1)])
                        P.op("act", lambda e, ci=ci: e.activation(out=PT[:, ci, :, :], in_=PSD[ci][:, :].rearrange("p (h q) -> p h q", h=8),
                                                                  func=AF.Exp, scale=0.125),
                             reads=[bk(2 * ci), bk(2 * ci + 1)], writes=[("PT", ci)])
                    P.op("pool", lambda e, msk=msk: e.tensor_tensor(out=PT[:, :, :, :], in0=PT[:, :, :, :],
                                                                   in1=apx(msk[:, :, :], 0, [[128, 2], [0, 8], [1, 128]]), op=ALU.mult),
                         reads=[("PT", 0), ("PT", 1), mk_], writes=[("PT", 0), ("PT", 1)])
                    for kvh in range(2):
                        rows = slice(64 * kvh, 64 * kvh + 64)
                        cols = slice(64 * kvh, 64 * kvh + 64)
                        P.group("pe", [
                            lambda e, kvh=kvh, rows=rows, cols=cols, vp=vp: e.matmul(bank(4 + kvh)[rows, :], lhsT=vp[:, cols], rhs=PT[:, 0, 4 * kvh:4 * kvh + 4, :],
                                                                             start=True, stop=False, tile_position=(0, 64 * kvh)),
                            lambda e, kvh=kvh, rows=rows, cols=cols, blk=blk: e.matmul(bank(4 + kvh)[rows, :], lhsT=vtok[:, blk, cols], rhs=PT[:, 1, 4 * kvh:4 * kvh + 4, :],
                                                                             start=False, stop=True, tile_position=(0, 64 * kvh))],
                            reads=[vpk, "vtok", ("PT", 0), ("PT", 1)], writes=[bk(4 + kvh)])
                        P.group("pe", [
                            lambda e, kvh=kvh, rows=rows: e.matmul(bank(6 + kvh)[rows, :], lhsT=onesb[:, 0:64], rhs=PT[:, 0, 4 * kvh:4 * kvh + 4, :],
                                                                   start=True, stop=False, tile_position=(0, 64 * kvh)),
                            lambda e, kvh=kvh, rows=rows: e.matmul(bank(6 + kvh)[rows, :], lhsT=onesb[:, 0:64], rhs=PT[:, 1, 4 * kvh:4 * kvh + 4, :],
                                                                   start=False, stop=True, tile_position=(0, 64 * kvh))],
                            reads=["onesb", ("PT", 0), ("PT", 1)], writes=[bk(6 + kvh)])
                    attn_finish(T, blk * 128, 128)
                P.op("pool", lambda e: e.tensor_copy(out=kprev[:, :], in_=U1[:, 20, T - 128:T]), reads=[U1k(20)], writes=["kprev"])
                P.op("pool", lambda e: e.tensor_copy(out=vprev[:, :], in_=vtok[:, nb - 1, :]), reads=["vtok"], writes=["vprev"])
            add(f)

        def attn_up(T):
            def lw(s):
                src = dram_ap(D["w_up"], s * 256 * 1024, [[1024, 64], [64 * 1024, 4], [1, 1024]])
                return (lambda slot, s=s: slot[64 * s:64 * s + 64, :].rearrange("p (j c) -> p j c", j=4), src)

            def f(slot, sk):
                for dc in range(8):
                    b = next_bank()
                    P.group("pe", [lambda e, j=j, dc=dc, b=b: e.matmul(bank(b)[:, :T], lhsT=slot[:, j * 1024 + dc * 128: j * 1024 + dc * 128 + 128],
                                                                     rhs=OT[:, j, :T], start=(j == 0), stop=(j == 3)) for j in range(4)],
                            reads=[sk, ("OT", 0), ("OT", 1)], writes=[bk(b)])
                    P.op("dve", lambda e, dc=dc, b=b: e.tensor_tensor(out=merged[:, dc, :T], in0=bank(b)[:, :T], in1=U1[:, dc, :T], op=ALU.mult),
                         reads=[bk(b), U1k(dc)], writes=[("mg", dc)])
            add(f, [lw(0), lw(1)])

        def ssm_X(tq, half, T):
            for ci in range(2):
                fns = []
                for pl in range(8):
                    pr = half * 8 + pl
                    ct, q = pr // 4, pr % 4
                    fns.append(lambda e, pl=pl, ct=ct, pr=pr, ci=ci: e.matmul(
                        PSD[ci][:, pl * 128:(pl + 1) * 128], lhsT=BbT[:, pr, ci, :], rhs=uT[:, ct, tq],
                        start=True, stop=True))
                P.group("pe", fns, reads=["BbT"] + [("uT", c) for c in range(4)], writes=[bk(2 * ci), bk(2 * ci + 1)])

        def ssm_prerot_scan(half, nj, nbat, inj=True):
            prs = slice(8 * half, 8 * half + 8)

            def tab(t):
                if nbat == 1:
                    return t[:, prs, :].rearrange("p a b -> p (a b)")
                return apx(t[:, prs, 0:nj], 0, [[128, 8], [0, nbat], [1, nj]])

            def v(ap):
                if nbat == 1:
                    return ap
                return ap.rearrange("p (a b j) -> p a b j", a=8, b=nbat)
            xr, xi = v(PSD[0][:, :]), v(PSD[1][:, :])
            a1, a2, b1, b2 = v(SS[:, 0, :]), v(SS[:, 1, :]), v(SS[:, 2, :]), v(SS[:, 3, :])
            c_, s_, r_ = tab(cosT), tab(sinT), tab(rmask)
            P.op("dve", lambda e: e.tensor_tensor(out=a1, in0=xr, in1=c_, op=ALU.mult), reads=[bk(0), bk(1), "rot"], writes=[("SS", 0)])
            P.op("dve", lambda e: e.tensor_tensor(out=a2, in0=xi, in1=s_, op=ALU.mult), reads=[bk(2), bk(3), "rot"], writes=[("SS", 1)])
            P.op("dve", lambda e: e.tensor_tensor(out=b1, in0=xi, in1=c_, op=ALU.mult), reads=[bk(2), bk(3), "rot"], writes=[("SS", 2)])
            P.op("dve", lambda e: e.tensor_tensor(out=b2, in0=xr, in1=s_, op=ALU.mult), reads=[bk(0), bk(1), "rot"], writes=[("SS", 3)])
            P.op("pool", lambda e: e.tensor_tensor(out=SS[:, 0, :], in0=SS[:, 0, :], in1=SS[:, 1, :], op=ALU.add),
                 reads=[("SS", 0), ("SS", 1)], writes=[("SS", 0)])
            P.op("pool", lambda e: e.tensor_tensor(out=SS[:, 1, :], in0=SS[:, 2, :], in1=SS[:, 3, :], op=ALU.subtract),
                 reads=[("SS", 2), ("SS", 3)], writes=[("SS", 1)])
            if inj:
                if nbat == 1:
                    for ci, idx in ((0, INJR), (1, INJI)):
                        P.op("dve", lambda e, ci=ci, idx=idx: e.tensor_tensor(
                            out=apx(SS[:, ci, :], 0, [[128, 8], [1, 1]]), in0=apx(SS[:, ci, :], 0, [[128, 8], [1, 1]]),
                            in1=apx(sp_[:, idx, prs], 0, [[1, 8], [1, 1]]), op=ALU.add), reads=[("SS", ci), SPK], writes=[("SS", ci)])
                else:
                    for ci in range(2):
                        P.op("dve", lambda e, ci=ci: e.tensor_tensor(
                            out=apx(SS[:, ci, :], 0, [[128, 8], [nj, nbat], [1, 1]]), in0=apx(SS[:, ci, :], 0, [[128, 8], [nj, nbat], [1, 1]]),
                            in1=apx(hsinj[:, ci, prs, :], 0, [[16, 8], [1, 16], [1, 1]]), op=ALU.add), reads=[("SS", ci), "hsinj"], writes=[("SS", ci)])
            if nbat == 1:
                rflat = r_
            else:
                rflat = hri[:, :, :].rearrange("p a b -> p (a b)").bitcast(F32)
                P.op("dve", lambda e: e.tensor_copy(out=v(rflat), in_=r_), reads=["rmask", ("hri", 0), ("hri", 1)], writes=[("hri", 0), ("hri", 1)])
            for ci in range(2):
                src = SS[:, ci, :]
                dst = SS[:, 2 + ci, :]
                P.op("dve", lambda e, src=src, dst=dst: e.tensor_tensor_scan(out=dst, data0=rflat, data1=src, initial=0.0,
                                                                          op0=ALU.mult, op1=ALU.add),
                     reads=[("SS", ci), "rmask", ("hri", 0), ("hri", 1)], writes=[("SS", 2 + ci)])

        def ssm_postrot_y(half, T, tq, nj, nbat):
            prs = slice(8 * half, 8 * half + 8)

            def tab(t):
                if nbat == 1:
                    return t[:, prs, :].rearrange("p a b -> p (a b)")
                return apx(t[:, prs, 0:nj], 0, [[128, 8], [0, nbat], [1, nj]])

            def v(ap):
                if nbat == 1:
                    return ap
                return ap.rearrange("p (a b j) -> p a b j", a=8, b=nbat)
            c_, s_ = tab(cosT), tab(sinT)
            gr, gi = v(SS[:, 2, :]), v(SS[:, 3, :])
            t0_, t1_ = v(SS[:, 0, :]), v(SS[:, 1, :])
            hre, him = v(hri[:, 0, :]), v(hri[:, 1, :])
            P.op("pool", lambda e: e.tensor_tensor(out=t0_, in0=gr, in1=c_, op=ALU.mult), reads=[("SS", 2), "rot"], writes=[("SS", 0)])
            P.op("dve", lambda e: e.tensor_tensor(out=t1_, in0=gi, in1=s_, op=ALU.mult), reads=[("SS", 3), "rot"], writes=[("SS", 1)])
            P.op("pool", lambda e: e.tensor_tensor(out=hri[:, 0, :], in0=SS[:, 0, :], in1=SS[:, 1, :], op=ALU.subtract),
                 reads=[("SS", 0), ("SS", 1)], writes=[("hri", 0)])
            P.op("pool", lambda e: e.tensor_tensor(out=t0_, in0=gi, in1=c_, op=ALU.mult), reads=[("SS", 3), "rot", ("hri", 0)], writes=[("SS", 0)])
            P.op("dve", lambda e: e.tensor_tensor(out=t1_, in0=gr, in1=s_, op=ALU.mult), reads=[("SS", 2), "rot", ("hri", 0)], writes=[("SS", 1)])
            P.op("pool", lambda e: e.tensor_tensor(out=hri[:, 1, :], in0=SS[:, 0, :], in1=SS[:, 1, :], op=ALU.add),
                 reads=[("SS", 0), ("SS", 1)], writes=[("hri", 1)])
            for cl in range(2):
                ct = half * 2 + cl
                b = 4 + (ct % 2)
                fns = []
                for q in range(4):
                    pr = ct * 4 + q
                    pl = pr % 8
                    for ci in range(2):
                        fns.append(lambda e, q=q, pr=pr, pl=pl, ci=ci, b=b: e.matmul(
                            bank(b)[32 * q:32 * q + 32, 0:128], lhsT=CE[:, pr, ci, :], rhs=hri[:, ci, pl * 128:(pl + 1) * 128],
                            start=(ci == 0), stop=(ci == 1), tile_position=(0, 32 * q)))
                P.group("pe", fns, reads=["CE", ("hri", 0), ("hri", 1)], writes=[bk(b)])
                P.op("dve", lambda e, ct=ct, b=b: e.scalar_tensor_tensor(out=rd[:, 0:128], in0=uT[:, ct, tq], scalar=dcol[:, ct:ct + 1],
                                                                        in1=bank(b)[:, 0:128], op0=ALU.mult, op1=ALU.add),
                     reads=[bk(b), ("uT", ct), "dcol", ("rd", 0), ("rd", 1)], writes=[("rd", 0), ("rd", 1)])
                P.op("act", lambda e, ct=ct: e.activation(out=zT[:, ct, tq], in_=rd[:, 0:128], func=AF.Gelu_apprx_tanh),
                     reads=[("rd", 0), ("rd", 1)], writes=[("zT", ct)])

        def ssm_carry(half):
            prs = slice(8 * half, 8 * half + 8)
            g = lambda ci: apx(SS[:, 2 + ci, :], 127, [[128, 8]])
            w = lambda idx: sp_[:, idx, prs]
            for ci, idx in ((0, G127R), (1, G127I)):
                P.op("dve", lambda e, ci=ci, idx=idx: e.tensor_copy(out=w(idx), in_=g(ci)), reads=[("SS", 2 + ci)], writes=[SPK])

        def cmul_half(or_, oi_, ar, ai, br, bi):
            cmul(or_, oi_, ar, ai, br, bi)

        def ssm_prompt(T, with_y):
            nb = T // 128

            def f(_s, _k):
                for blk in range(nb):
                    tq = slice(blk * 128, (blk + 1) * 128)
                    for half in range(2):
                        ssm_X(tq, half, T)
                        ssm_prerot_scan(half, 128, 1)
                        ssm_carry(half)
                        if with_y:
                            ssm_postrot_y(half, T, tq, 128, 1)
                    cmul(INJR, INJI, W128R, W128I, G127R, G127I)
            add(f)

        def ssm_final_state():
            cmul(HRE, HIM, C127, S127, G127R, G127I)

        import os as _os2
        GLU_ENG = _os2.environ.get('GLU_ENG', 'dve')

        def glu_merge(T):
            for pi in range(2):
                def lg(ab, pi=pi):
                    src = dram_ap(D["w_glu"], ab * 1024 + pi * 512, [[2048, 128], [128 * 2048, 4], [1, 512]])
                    return (lambda slot, ab=ab: apx(slot[:, :], ab * 512, [[1024, 4], [1, 512]]), src)

                def f(slot, sk, pi=pi):
                    for dl in range(4):
                        dc = pi * 4 + dl
                        ba, bb = next_bank(), next_bank()
                        for (b, ab) in ((ba, 0), (bb, 1)):
                            P.group("pe", [lambda e, kc=kc, b=b, ab=ab, dl=dl: e.matmul(
                                bank(b)[:, :T], lhsT=slot[:, kc * 1024 + ab * 512 + dl * 128: kc * 1024 + ab * 512 + dl * 128 + 128],
                                rhs=zT[:, kc, :T], start=(kc == 0), stop=(kc == 3)) for kc in range(4)],
                                reads=[sk] + [("zT", c) for c in range(4)], writes=[bk(b)])
                        P.op("act", lambda e, bb=bb: e.activation(out=tmpb[:, 0, :T], in_=bank(bb)[:, :T], func=AF.Sigmoid),
                             reads=[bk(bb)], writes=[("tmpb", 0)])
                        P.op("dve", lambda e, ba=ba: e.tensor_tensor(out=tmpb[:, 1, :T], in0=bank(ba)[:, :T], in1=tmpb[:, 0, :T], op=ALU.mult),
                             reads=[bk(ba), ("tmpb", 0)], writes=[("tmpb", 1)])
                        P.op(GLU_ENG, lambda e, dc=dc: e.tensor_tensor(out=tmpb[:, 1, :T], in0=tmpb[:, 1, :T], in1=U1[:, 8 + dc, :T], op=ALU.mult),
                             reads=[("tmpb", 1), U1k(8 + dc)], writes=[("tmpb", 1)])
                        P.op(GLU_ENG, lambda e, dc=dc: e.tensor_tensor(out=merged[:, dc, :T], in0=merged[:, dc, :T], in1=tmpb[:, 1, :T], op=ALU.add),
                             reads=[("tmpb", 1), ("mg", dc)], writes=[("mg", dc)])
                add(f, [lg(0), lg(1)])

        def xattn_q(T):
            def evac(i, ps, pk):
                P.op("act", lambda e, i=i, ps=ps: e.copy(out=U1[:, i, :T], in_=ps), reads=[pk], writes=[U1k(i)])
            proj_fm(T, D["w_xq"], 0, 8, lambda kc: hT[:, kc, :], lambda kc: ("hT", kc), 8, evac)

        def xattn_prompt(T):
            def f(_s, _k):
                for hh in range(4):
                    for kc in range(2):
                        P.group("pe", [lambda e, dd=dd, kc=kc, hh=hh: e.matmul(bank(kc)[:, :T], lhsT=mkT[:, 2 * hh + dd, kc * 128:(kc + 1) * 128],
                                                                             rhs=U1[:, 2 * hh + dd, :T], start=(dd == 0), stop=(dd == 1)) for dd in range(2)],
                                reads=["mkT", U1k(2 * hh), U1k(2 * hh + 1)], writes=[bk(kc)])
                        P.op("act", lambda e, kc=kc: e.activation(out=tmpb[:, kc, :T], in_=bank(kc)[:, :T], func=AF.Exp, scale=1.0 / 16.0),
                             reads=[bk(kc)], writes=[("tmpb", kc)])
                    P.group("pe", [lambda e, kc=kc: e.matmul(bank(2)[:, :T], lhsT=onesb[:, :], rhs=tmpb[:, kc, :T], start=(kc == 0), stop=(kc == 1))
                                   for kc in range(2)], reads=["onesb", ("tmpb", 0), ("tmpb", 1)], writes=[bk(2)])
                    P.op("dve", lambda e: e.reciprocal(out=rd[:, :T], in_=bank(2)[:, :T]), reads=[bk(2)], writes=[("rd", 0), ("rd", 1)])
                    for dd in range(2):
                        b = 4 + dd
                        P.group("pe", [lambda e, kc=kc, dd=dd, hh=hh, b=b: e.matmul(bank(b)[:, :T], lhsT=mvp[:, kc, (2 * hh + dd) * 128:(2 * hh + dd + 1) * 128],
                                                                                   rhs=tmpb[:, kc, :T], start=(kc == 0), stop=(kc == 1)) for kc in range(2)],
                                reads=["mvp", ("tmpb", 0), ("tmpb", 1)], writes=[bk(b)])
                        P.op("dve", lambda e, dd=dd, hh=hh, b=b: e.tensor_tensor(out=U1[:, 8 + 2 * hh + dd, :T], in0=bank(b)[:, :T], in1=rd[:, :T], op=ALU.mult),
                             reads=[bk(b), ("rd", 0), ("rd", 1)], writes=[U1k(8 + 2 * hh + dd)])
            add(f)

        hsinj = sb("hsinj", [128, 2, 16, 16], F32)
        h0s = sb("h0s", [128, 2, 16, 16], F32)
        kvs = sb("kvs", [128, 2, 2, 1024], BF16)
        KcT = kvs[:, 0, :, :].rearrange("p a (b c) -> p (a b) c", c=128)
        Vc = kvs[:, 1, :, :].rearrange("p a (b c) -> p (a b) c", c=128)
        KbT = OT[:, :, :].rearrange("p a (b c) -> p (a b) c", c=256)
        PTs = sb("PTs", [128, 1024], BF16)
        PTe = tmpb[:, :, :].rearrange("p k (j q) -> p k j q", j=4)

        def sample_states_in():
            def f(_s, _k):
                for ci, nm in enumerate(("sre", "sim")):
                    for hf in range(2):
                        for pl in range(8):
                            P.dma("sp", SS[16 * pl:16 * pl + 16, hf, 0:128],
                                  dram_ap(D[nm], (hf * 8 + pl) * 128, [[2048, 16], [1, 128]]), writes=[("SS", hf)])
                        P.group("pe", [lambda e, hf=hf: e.transpose(bank(6)[:, 0:128], SS[:, hf, 0:128], identf[:, :])],
                                reads=[("SS", hf), "identf"], writes=[bk(6)])
                        P.op("dve", lambda e, ci=ci, hf=hf: e.tensor_copy(out=h0s[:, ci, 8 * hf:8 * hf + 8, :],
                                                                        in_=bank(6)[:, 0:128].rearrange("p (a b) -> p a b", a=8)),
                             reads=[bk(6)], writes=["h0s"])
                lr = apx(sp_[:, LRE, :], 0, [[1, 16], [0, 16]])
                li = apx(sp_[:, LIM, :], 0, [[1, 16], [0, 16]])
                ta = SS[:, 2, 0:256].rearrange("p (a b) -> p a b", a=16)
                tb_ = SS[:, 2, 256:512].rearrange("p (a b) -> p a b", a=16)
                P.op("dve", lambda e: e.tensor_tensor(out=ta, in0=h0s[:, 0, :, :], in1=lr, op=ALU.mult), reads=["h0s", SPK], writes=[("SS", 2)])
                P.op("dve", lambda e: e.tensor_tensor(out=tb_, in0=h0s[:, 1, :, :], in1=li, op=ALU.mult), reads=["h0s", SPK], writes=[("SS", 2)])
                P.op("dve", lambda e: e.tensor_tensor(out=hsinj[:, 0, :, :], in0=ta, in1=tb_, op=ALU.subtract), reads=[("SS", 2)], writes=["hsinj"])
                P.op("dve", lambda e: e.tensor_tensor(out=ta, in0=h0s[:, 0, :, :], in1=li, op=ALU.mult), reads=["h0s", SPK, "hsinj"], writes=[("SS", 2)])
                P.op("dve", lambda e: e.tensor_tensor(out=tb_, in0=h0s[:, 1, :, :], in1=lr, op=ALU.mult), reads=["h0s", SPK], writes=[("SS", 2)])
                P.op("dve", lambda e: e.tensor_tensor(out=hsinj[:, 1, :, :], in0=ta, in1=tb_, op=ALU.add), reads=[("SS", 2)], writes=["hsinj"])
            add(f)

        def ssm_sample():
            def f(_s, _k):
                tq = slice(0, 128)
                for half in range(2):
                    prs = slice(8 * half, 8 * half + 8)
                    ssm_X(tq, half, 128)
                    ssm_prerot_scan(half, 8, 16)
                    g7 = lambda ci: apx(SS[:, 2 + ci, :], 7, [[128, 8], [8, 16]])
                    c7 = apx(cosT[:, prs, 7:8], 0, [[128, 8], [0, 16]])
                    s7 = apx(sinT[:, prs, 7:8], 0, [[128, 8], [0, 16]])
                    ta = SS[:, 0, 0:128].rearrange("p (a b) -> p a b", a=8)
                    tb_ = SS[:, 0, 128:256].rearrange("p (a b) -> p a b", a=8)
                    P.op("dve", lambda e, c7=c7: e.tensor_tensor(out=ta, in0=g7(0), in1=c7, op=ALU.mult), reads=[("SS", 2), "rot"], writes=[("SS", 0)])
                    P.op("dve", lambda e, s7=s7: e.tensor_tensor(out=tb_, in0=g7(1), in1=s7, op=ALU.mult), reads=[("SS", 3), "rot"], writes=[("SS", 0)])
                    P.op("dve", lambda e, prs=prs: e.tensor_tensor(out=h0s[:, 0, prs, :], in0=ta, in1=tb_, op=ALU.subtract), reads=[("SS", 0)], writes=["h0s"])
                    P.op("dve", lambda e, c7=c7: e.tensor_tensor(out=ta, in0=g7(1), in1=c7, op=ALU.mult), reads=[("SS", 3), "rot", "h0s"], writes=[("SS", 0)])
                    P.op("dve", lambda e, s7=s7: e.tensor_tensor(out=tb_, in0=g7(0), in1=s7, op=ALU.mult), reads=[("SS", 2), "rot"], writes=[("SS", 0)])
                    P.op("dve", lambda e, prs=prs: e.tensor_tensor(out=h0s[:, 1, prs, :], in0=ta, in1=tb_, op=ALU.add), reads=[("SS", 0)], writes=["h0s"])
                    ssm_postrot_y(half, 128, tq, 8, 16)
                for ci, nm in enumerate(("nsr_s", "nsi_s")):
                    for hf in range(2):
                        P.op("dve", lambda e, ci=ci, hf=hf: e.tensor_copy(out=SS[:, 0, 0:128].rearrange("p (a b) -> p a b", a=8),
                                                                        in_=h0s[:, ci, 8 * hf:8 * hf + 8, :]), reads=["h0s"], writes=[("SS", 0)])
                        P.group("pe", [lambda e: e.transpose(bank(6)[:, 0:128], SS[:, 0, 0:128], identf[:, :])],
                                reads=[("SS", 0), "identf"], writes=[bk(6)])
                        P.op("dve", lambda e: e.tensor_copy(out=SS[:, 1, 0:128], in_=bank(6)[:, 0:128]), reads=[bk(6)], writes=[("SS", 1)])
                        for pl in range(8):
                            P.dma("sp", dram_ap(D[nm], (hf * 8 + pl) * 128, [[2048, 16], [1, 128]]), SS[16 * pl:16 * pl + 16, 1, 0:128], reads=[("SS", 1)])
            add(f)

        def attn_sample():
            def f(_s, _k):
                P.dma("pool", PT[:, :, :, :].rearrange("p a h q -> p (a h) q"), D["cwk"].rearrange("b k c -> k b c"), writes=[("PT", 0), ("PT", 1)])
                P.dma("pool", Vc, D["cwv"].rearrange("b k c -> k b c"), writes=["kvsv"])
                PTk = PT[:, :, :, :].rearrange("p a h q -> p (a h) q")
                pbf = [PSD[2][:, :].bitcast(BF16), PSD[3][:, :].bitcast(BF16)]
                for g in range(2):
                    P.group("pe", [lambda e, b_=b_, g=g: e.transpose(pbf[g][:, b_ * 128:(b_ + 1) * 128], PTk[:, g * 8 + b_, :], identb[:, :])
                                   for b_ in range(8)], reads=[("PT", 0), ("PT", 1), "identb"], writes=[bk(4 + 2 * g), bk(5 + 2 * g)])
                    P.op("dve", lambda e, g=g: e.tensor_copy(out=KcT[:, 8 * g:8 * g + 8, :], in_=pbf[g][:, 0:1024].rearrange("p (a b) -> p a b", a=8)),
                         reads=[bk(4 + 2 * g), bk(5 + 2 * g)], writes=["kvs"])
                fns = []
                for b_ in range(16):
                    for kvh in range(2):
                        rows = slice(64 * kvh, 64 * kvh + 64)
                        fns.append(lambda e, b_=b_, kvh=kvh, rows=rows: e.matmul(
                            PSD[0][:, kvh * 512 + b_ * 32: kvh * 512 + b_ * 32 + 32], lhsT=KcT[rows, b_, :], rhs=U1[rows, 16:20, b_ * 8:(b_ + 1) * 8],
                            start=True, stop=True, tile_position=(64 * kvh, 0)))
                P.group("pe", fns, reads=["kvs"] + qTk, writes=[bk(0), bk(1)])
                P.op("act", lambda e: e.activation(out=PTs[:, :], in_=PSD[0][:, :], func=AF.Exp, scale=0.125), reads=[bk(0), bk(1)], writes=["PTs"])
                P.op("pool", lambda e: e.tensor_tensor(out=PTs[:, :].rearrange("p (x q) -> p x q", q=8), in0=PTs[:, :].rearrange("p (x q) -> p x q", q=8),
                                                      in1=apx(maskctx[:, :], 0, [[0, 128], [1, 8]]), op=ALU.mult), reads=["PTs", "maskctx"], writes=["PTs"])
                fns = []
                for kvh in range(2):
                    rows = slice(64 * kvh, 64 * kvh + 64)
                    fns.append(lambda e, kvh=kvh, rows=rows: e.matmul(PSD[1][:, kvh * 512:(kvh + 1) * 512], lhsT=U1[rows, 20, 0:128],
                                                                      rhs=U1[rows, 16:20, 0:128], start=True, stop=True, tile_position=(64 * kvh, 0)))
                P.group("pe", fns, reads=[U1k(20)] + qTk, writes=[bk(2), bk(3)])
                P.op("act", lambda e: e.activation(out=PTe, in_=PSD[1][:, :].rearrange("p (k j q) -> p k j q", k=2, j=4), func=AF.Exp, scale=0.125),
                     reads=[bk(2), bk(3)], writes=[("tmpb", 0), ("tmpb", 1)])
                P.op("pool", lambda e: e.tensor_tensor(out=tmpb[:, :, :].rearrange("p k (j q) -> p (k j) q", j=4), in0=tmpb[:, :, :].rearrange("p k (j q) -> p (k j) q", j=4),
                                                      in1=apx(masksc[:, :], 0, [[0, 8], [1, 128]]), op=ALU.mult), reads=[("tmpb", 0), ("tmpb", 1), "masksc"], writes=[("tmpb", 0), ("tmpb", 1)])
                for kvh in range(2):
                    rows = slice(64 * kvh, 64 * kvh + 64)
                    cols = slice(64 * kvh, 64 * kvh + 64)
                    for (bnk, lh_cur, lh_ctx) in ((4 + kvh, lambda cols=cols: vtok[:, 0, cols], lambda b_, cols=cols: Vc[:, b_, cols]),
                                                  (6 + kvh, lambda: onesb[:, 0:64], lambda b_: onesb[:, 0:64])):
                        fns = [lambda e, bnk=bnk, lh_cur=lh_cur, kvh=kvh, rows=rows: e.matmul(
                            bank(bnk)[rows, :], lhsT=lh_cur(), rhs=PTe[:, kvh, :, :], start=True, stop=False, tile_position=(0, 64 * kvh))]
                        for b_ in range(16):
                            fns.append(lambda e, bnk=bnk, lh_ctx=lh_ctx, b_=b_, kvh=kvh, rows=rows: e.matmul(
                                apx(bank(bnk)[rows, :], b_ * 8, [[128, 4], [1, 8]]), lhsT=lh_ctx(b_),
                                rhs=PTs[:, kvh * 512 + b_ * 32: kvh * 512 + b_ * 32 + 32], start=False, stop=(b_ == 15), tile_position=(0, 64 * kvh)))
                        P.group("pe", fns, reads=["vtok", "kvsv", "onesb", ("tmpb", 0), ("tmpb", 1), "PTs"], writes=[bk(bnk)])
                attn_finish(128, 0, 128)
                P.dma("sp", D["nwk_s"][:, 0:120, :], D["cwk"][:, 8:128, :])
                P.dma("sp", D["nwv_s"][:, 0:120, :], D["cwv"][:, 8:128, :])
                P.group("pe", [lambda e: e.transpose(bank(0)[:, 0:128], krof[:, :], identf[:, :])], reads=["krof", "identf"], writes=[bk(0)])
                P.op("dve", lambda e: e.tensor_copy(out=SS[:, 0, 0:128], in_=bank(0)[:, 0:128]), reads=[bk(0)], writes=[("SS", 0)])
                P.dma("sp", D["nwk_s"][:, 120:128, :], SS[:, 0, 0:128], reads=[("SS", 0)])
                P.dma("sp", D["nwv_s"][:, 120:128, :], vtokf[:, :], reads=["vtokf"])
            add(f)

        def xattn_sample():
            def f(_s, _k):
                pbf = [PSD[2][:, :].bitcast(BF16), PSD[3][:, :].bitcast(BF16)]
                for b_ in range(16):
                    P.dma("pool", kvs[:, 0, :, :], D["cmk"][b_].rearrange("(kb p) c -> p kb c", p=128), writes=["kvs"])
                    for kb in range(2):
                        P.group("pe", [lambda e, dc=dc, kb=kb: e.transpose(pbf[kb][:, dc * 128:(dc + 1) * 128], kvs[:, 0, kb, dc * 128:(dc + 1) * 128], identb[:, :])
                                       for dc in range(8)], reads=["kvs", "identb"], writes=[bk(4 + 2 * kb), bk(5 + 2 * kb)])
                        P.op("dve" if kb == 0 else "act",
                             (lambda e, kb=kb: e.tensor_copy(out=KbT[:, :, kb * 128:(kb + 1) * 128], in_=pbf[kb][:, 0:1024].rearrange("p (a b) -> p a b", a=8))) if kb == 0 else
                             (lambda e, kb=kb: e.copy(out=KbT[:, :, kb * 128:(kb + 1) * 128], in_=pbf[kb][:, 0:1024].rearrange("p (a b) -> p a b", a=8))),
                             reads=[bk(4 + 2 * kb), bk(5 + 2 * kb)], writes=[("OT", 0), ("OT", 1)])
                    fns = []
                    for hh in range(4):
                        for kb in range(2):
                            for dd in range(2):
                                fns.append(lambda e, hh=hh, kb=kb, dd=dd, b_=b_: e.matmul(
                                    bank(0)[:, kb * 32 + hh * 8: kb * 32 + hh * 8 + 8], lhsT=KbT[:, 2 * hh + dd, kb * 128:(kb + 1) * 128],
                                    rhs=U1[:, 2 * hh + dd, b_ * 8:(b_ + 1) * 8], start=(dd == 0), stop=(dd == 1)))
                    P.group("pe", fns, reads=[("OT", 0), ("OT", 1)] + [U1k(i) for i in range(8)], writes=[bk(0)])
                    P.op("act", lambda e: e.activation(out=PTs[:, 0:64], in_=bank(0)[:, 0:64], func=AF.Exp, scale=1.0 / 16.0), reads=[bk(0)], writes=["PTs"])
                    P.dma("pool", kvs[:, 1, :, :], D["cmv"][b_].rearrange("(kb p) c -> p kb c", p=128), writes=["kvsv"])
                    P.group("pe", [lambda e, kb=kb: e.matmul(bank(1)[:, 0:32], lhsT=onesb[:, :], rhs=PTs[:, kb * 32:(kb + 1) * 32], start=(kb == 0), stop=(kb == 1))
                                   for kb in range(2)], reads=["onesb", "PTs"], writes=[bk(1)])
                    P.op("dve", lambda e: e.reciprocal(out=rd[:, 0:32], in_=bank(1)[:, 0:32]), reads=[bk(1)], writes=[("rd", 0), ("rd", 1)])
                    fns = []
                    for dc in range(8):
                        hh = dc // 2
                        for kb in range(2):
                            fns.append(lambda e, dc=dc, hh=hh, kb=kb: e.matmul(
                                bank(2)[:, dc * 8:(dc + 1) * 8], lhsT=kvs[:, 1, kb, dc * 128:(dc + 1) * 128],
                                rhs=PTs[:, kb * 32 + hh * 8: kb * 32 + hh * 8 + 8], start=(kb == 0), stop=(kb == 1)))
                    P.group("pe", fns, reads=["kvsv", "PTs"], writes=[bk(2)])
                    P.op("dve", lambda e, b_=b_: e.tensor_tensor(out=U1[:, 8:16, b_ * 8:(b_ + 1) * 8].rearrange("p (h d) q -> p h d q", h=4),
                                                               in0=bank(2)[:, 0:64].rearrange("p (h d q) -> p h d q", h=4, d=2),
                                                               in1=apx(rd[:, 0:32], 0, [[8, 4], [0, 2], [1, 8]]), op=ALU.mult),
                         reads=[bk(2), ("rd", 0), ("rd", 1)], writes=[U1k(8 + i) for i in range(8)])
            add(f)

        u1src = lambda kc: U1[:, 8 + kc, :]
        u1srck = lambda kc: U1k(8 + kc)
        mgsrc = lambda kc: merged[:, kc, :]
        mgsrck = lambda kc: ("mg", kc)

        load_x(128, D["xh"])
        norm(128, 0)
        ffn(128, D["w1"], D["w1o"])
        norm(128, 1)
        load_rope(128, TP)
        inproj_kv(128, False)

        def halo_fin(_s, _k):
            P.op("pool", lambda e: e.tensor_copy(out=kprev[:, :], in_=U1[:, 20, 0:128]), reads=[U1k(20)], writes=["kprev"])
            P.op("pool", lambda e: e.tensor_copy(out=vprev[:, :], in_=vtok[:, 0, :]), reads=["vtok"], writes=["vprev"])
            P.op("dve", lambda e: e.memset(sp_[:, INJR:INJI + 1, :], 0.0), reads=[SPK], writes=[SPK])
        add(halo_fin)

        for t in range(4):
            load_x(512, D["xp"][t * 512:(t + 1) * 512, :])
            norm(512, 0)
            ffn(512, D["w1"], D["w1o"])

            def spill(_s, _k, t=t):
                P.dma("sp", x1d[:, :, t * 512:(t + 1) * 512].rearrange("k p t -> p k t"), xT[:, :, :],
                      reads=[("xT", kc) for kc in range(8)], writes=[("x1d", t)])
            add(spill)
            norm(512, 1)
            inproj_u(512)
            ssm_prompt(512, False)

        def exchange(_s, _k):
            ssm_final_state()
            P.op("dve", lambda e: e.tensor_copy(out=fall[:, 0, 0:16], in_=V(HRE)), reads=[SPK], writes=["fall"])
            P.op("dve", lambda e: e.tensor_copy(out=fall[:, 0, 16:32], in_=V(HIM)), reads=[SPK], writes=["fall"])
            P.dma("sp", ag_in, fall[:, 0, :], reads=["fall"], writes=["ag_in"])
            P.custom("pool", lambda e: e.collective_compute("AllGather", ALU.bypass, replica_groups=[list(range(NCORE))],
                                                            ins=[ag_in], outs=[ag_out]), "ccs", reads=["ag_in"], writes=["ag_out"])
        add(exchange)

        load_x(128, D["xs"])
        norm(128, 0)
        ffn(128, D["w1"], D["w1o"])
        norm(128, 1)
        load_rope(128, TP + 128)
        inproj_q(128)
        inproj_kv(128, True)
        inproj_u(128)
        inproj_gates(128)
        attn_sample()
        attn_up(128)
        sample_states_in()
        ssm_sample()
        glu_merge(128)
        resid_proj(128, D["w_out"], mgsrc, mgsrck)
        norm(128, 2)
        xattn_q(128)
        xattn_sample()
        resid_proj(128, D["w_xo"], u1src, u1srck)
        norm(128, 3)
        ffn(128, D["w2"], D["w2o"])
        store_y(128, D["ys"])

        def chain(_s, _k):
            P.dma("sp", fall[:, :, :], ag_out.rearrange("(r p) c -> p r c", p=128), reads=["ag_out"], writes=["fall"])
            P.op("dve", lambda e: e.memset(sp_[:, ACCR:ACCI + 1, :], 0.0), reads=[SPK], writes=[SPK])
            for c in range(NCORE - 1):
                cmul(T4, T5, L2KR, L2KI, ACCR, ACCI)
                P.op("dve", lambda e, c=c: e.tensor_tensor(out=V(T4), in0=V(T4), in1=fall[:, c, 0:16], op=ALU.add), reads=[SPK, "fall"], writes=[SPK])
                P.op("dve", lambda e, c=c: e.tensor_tensor(out=V(T5), in0=V(T5), in1=fall[:, c, 16:32], op=ALU.add), reads=[SPK, "fall"], writes=[SPK])
                tt(T4, T4, ACCR, ALU.subtract)
                tt(T5, T5, ACCI, ALU.subtract)
                for acc, tmp in ((ACCR, T4), (ACCI, T5)):
                    P.op("dve", lambda e, c=c, acc=acc, tmp=tmp: e.scalar_tensor_tensor(out=V(acc), in0=V(tmp), scalar=cmask[:, c:c + 1], in1=V(acc),
                                                                                      op0=ALU.mult, op1=ALU.add), reads=[SPK, "cmask"], writes=[SPK])
            cmul(INJR, INJI, LRE, LIM, ACCR, ACCI)
        add(chain)

        for t in range(4):
            def reload(_s, _k, t=t):
                P.dma("sp", xT[:, :, :], x1d[:, :, t * 512:(t + 1) * 512].rearrange("k p t -> p k t"),
                      reads=[("x1d", t)], writes=[("xT", kc) for kc in range(8)])
            add(reload)
            norm(512, 1)
            load_rope(512, t * 512)
            inproj_q(512)
            inproj_kv(512, t == 3)
            inproj_u(512)
            inproj_gates(512)
            attn_prompt(512, maskh if t == 0 else maskp)
            attn_up(512)
            ssm_prompt(512, True)
            glu_merge(512)
            resid_proj(512, D["w_out"], mgsrc, mgsrck)
            norm(512, 2)
            xattn_q(512)
            xattn_prompt(512)
            resid_proj(512, D["w_xo"], u1src, u1srck)
            norm(512, 3)
            ffn(512, D["w2"], D["w2o"])
            store_y(512, D["yp"][t * 512:(t + 1) * 512, :])

        def prompt_outs(_s, _k):
            P.group("pe", [lambda e: e.transpose(bank(0)[:, 0:128], krof[:, :], identf[:, :])], reads=["krof", "identf"], writes=[bk(0)])
            P.op("dve", lambda e: e.tensor_copy(out=SS[:, 0, 0:128], in_=bank(0)[:, 0:128]), reads=[bk(0)], writes=[("SS", 0)])
            P.dma("sp", D["nwk_p"], SS[:, 0, 0:128], reads=[("SS", 0)])
            P.dma("sp", D["nwv_p"], vtokf[:, :], reads=["vtokf"])
            ssm_final_state()
            for idx, nm in ((HRE, "nsr_p"), (HIM, "nsi_p")):
                P.op("dve", lambda e, idx=idx: e.tensor_copy(out=SS[:, 1, 0:16], in_=V(idx)), reads=[SPK, ("SS", 1)], writes=[("SS", 1)])
                for two in range(2):
                    P.dma("sp", dram_ap(D[nm], two * 64, [[1, 64], [128, 16]]), SS[64 * two:64 * two + 64, 1, 0:16], reads=[("SS", 1)],
                          allow_slow_non_contiguous=True)
        add(prompt_outs)

        import os as _os
        _ks = int(_os.environ.get("KSTOP", "0"))
        if _ks > 0:
            del items[_ks:]
        flush()
        P.wait_all("sp")
        print('CNT', P.cnt, {k: v[1] for k, v in P.dma_sems.items()}, {k: len(v) for k, v in P.q.items()}, flush=True)
        if _os.environ.get("KSIM"):
            sem = {}
            pos = {e: 0 for e in ENGS}
            prog = True
            while prog:
                prog = False
                for e in ENGS:
                    tr = P.trace[e]
                    while pos[e] < len(tr):
                        kind, k, v = tr[pos[e]]
                        if kind == "wait":
                            if sem.get(k, 0) >= v:
                                pos[e] += 1; prog = True
                            else:
                                break
                        else:
                            sem[k] = sem.get(k, 0) + v
                            pos[e] += 1; prog = True
            for e in ENGS:
                print("SIM", e, pos[e], len(P.trace[e]), P.trace[e][pos[e]] if pos[e] < len(P.trace[e]) else None, flush=True)
        P.replay(block)
    return nc


def _host_consts(core):
    half = 32
    inv = (np.float32(10000.0) ** (-np.arange(half, dtype=np.float32) / np.float32(half))).astype(np.float32)
    pos = np.concatenate([
        np.arange(core * TP, (core + 1) * TP, dtype=np.float32),
        np.arange(core * TP - 128, core * TP, dtype=np.float32),
        np.tile(np.arange(16384, 16392, dtype=np.float32), 16),
    ])
    ang = (pos[None, :] * inv[:, None]).astype(np.float32)
    cos = np.cos(ang).astype(np.float32)
    sin = np.sin(ang).astype(np.float32)
    c64 = np.concatenate([cos, cos], 0)
    s64 = np.concatenate([-sin, sin], 0)
    ropec = np.concatenate([c64, c64], 0)
    ropes = np.concatenate([s64, s64], 0)
    cmask = np.tile((np.arange(8) < core).astype(np.float32)[None, :], (128, 1))
    c = np.arange(128)[:, None]
    r = np.arange(128)[None, :]
    mctx = (c >= r).astype(np.float32)
    mcur = (c <= r).astype(np.float32)
    maskp = np.concatenate([mctx, mcur], 1)
    maskh = np.concatenate([mctx * (1.0 if core > 0 else 0.0), mcur], 1)
    same = (c // 8) == (r // 8)
    masksc = (same & (c <= r)).astype(np.float32)
    maskctx = (np.arange(128)[:, None] >= np.arange(8)[None, :]).astype(np.float32)
    identf = np.eye(128, dtype=np.float32)
    perm = np.zeros((128, 128), np.float32)
    for m in range(128):
        k = m + 32 if (m % 64) < 32 else m - 32
        perm[k, m] = 1.0
    jidx = np.tile(np.arange(128, dtype=np.float32)[None, :], (128, 1))
    return dict(ropec=np.ascontiguousarray(ropec), ropes=np.ascontiguousarray(ropes), cmask=cmask, maskh=maskh, maskp=maskp,
                masksc=masksc, maskctx=maskctx, identf=identf, permf=perm, jidx=jidx)


_NC_CACHE = {}


def kernel(x_prompt, x_sample, cache_win_k, cache_win_v, state_ssm_re, state_ssm_im,
           cache_mem_k, cache_mem_v, mem_prompt,
           g_ffn1, w_ffn1_in, w_ffn1_out, g_mix, w_in, attn_sinks,
           ssm_a_re, ssm_a_im, ssm_log_dt, ssm_b_re, ssm_b_im, ssm_c_re, ssm_c_im, ssm_d,
           w_attn_up, w_ssm_glu, w_out, g_xattn, g_mem, w_xq, w_xk, w_xv, w_xo,
           g_ffn2, w_ffn2_in, w_ffn2_out, g_final):
    f = lambda a: np.ascontiguousarray(np.asarray(a, dtype=np.float32))
    xp = f(x_prompt)[0]
    xs = f(x_sample).reshape(128 * 8, 1024)
    shared = dict(
        mem=f(mem_prompt)[0], g_ffn1=f(g_ffn1)[0], w1=f(w_ffn1_in)[0], w1o=f(w_ffn1_out)[0], g_mix=f(g_mix)[0], w_in=f(w_in)[0],
        sinks=f(attn_sinks)[0], a_re=f(ssm_a_re)[0], a_im=f(ssm_a_im)[0], log_dt=f(ssm_log_dt)[0], b_re=f(ssm_b_re)[0],
        b_im=f(ssm_b_im)[0], c_re=f(ssm_c_re)[0], c_im=f(ssm_c_im)[0], ssm_d=f(ssm_d)[0], w_up=f(w_attn_up)[0],
        w_glu=f(w_ssm_glu)[0], w_out=f(w_out)[0], g_x=f(g_xattn)[0], g_mem=f(g_mem)[0], w_xq=f(w_xq)[0], w_xk=f(w_xk)[0],
        w_xv=f(w_xv)[0], w_xo=f(w_xo)[0], g_ffn2=f(g_ffn2)[0], w2=f(w_ffn2_in)[0], w2o=f(w_ffn2_out)[0], g_final=f(g_final))
    cwk = f(cache_win_k)[0].reshape(128, 128, 128)
    cwv = f(cache_win_v)[0].reshape(128, 128, 128)
    sre = f(state_ssm_re)[0]
    sim = f(state_ssm_im)[0]
    cmk = f(cache_mem_k)[0].reshape(128, 256, 1024)
    cmv = f(cache_mem_v)[0].reshape(128, 256, 1024)
    in_maps = []
    for c in range(NCORE):
        m = dict(shared)
        m["xp"] = xp[c * TP:(c + 1) * TP]
        m["xh"] = xp[c * TP - 128:c * TP] if c > 0 else xp[0:128]
        m["xs"] = xs[c * 128:(c + 1) * 128]
        sl = slice(16 * c, 16 * c + 16)
        m["cwk"], m["cwv"], m["sre"], m["sim"], m["cmk"], m["cmv"] = cwk[sl], cwv[sl], sre[sl], sim[sl], cmk[sl], cmv[sl]
        m.update(_host_consts(c))
        in_maps.append(m)
    if "nc" not in _NC_CACHE:
        _NC_CACHE["nc"] = build_nc()
    res = run_bass_kernel_spmd(_NC_CACHE["nc"], in_maps, core_ids=list(range(NCORE)))
    R = res.results
    cat = lambda k: np.concatenate([R[c][k] for c in range(NCORE)], 0)
    y_prompt = cat("yp").reshape(1, 16384, 1024)
    y_sample = cat("ys").reshape(128, 8, 1024)
    return (y_prompt.astype(np.float32), y_sample.astype(np.float32),
            R[7]["nwk_p"].reshape(1, 1, 128, 2, 64), R[7]["nwv_p"].reshape(1, 1, 128, 2, 64),
            R[7]["nsr_p"].reshape(1, 1, 32, 64), R[7]["nsi_p"].reshape(1, 1, 32, 64),
            R[0]["nmk_p"].reshape(1, 1, 256, 4, 256), R[0]["nmv_p"].reshape(1, 1, 256, 4, 256),
            cat("nwk_s").reshape(1, 128, 128, 2, 64), cat("nwv_s").reshape(1, 128, 128, 2, 64),
            cat("nsr_s").reshape(1, 128, 32, 64), cat("nsi_s").reshape(1, 128, 32, 64))
```
